# Optimizing a Trainium2 kernel written in Bass

```python
import jax, jax.numpy as jnp
from jax import lax
import numpy as np

D_MODEL = 2048
BATCH = 16
SEQ = 256
DEPTH = 2
DEC_BATCH = 4
DEC_SEQ = 1024
PAST_LEN = 256

GRID_W = 64
N_HEADS = 16
HEAD_DIM = 64
D_NA = N_HEADS * HEAD_DIM
D_F = D_MODEL // 2
N_FGROUPS = 4
F_GROUP = D_F // N_FGROUPS
KH_MAX = 8
KW = 16
Q_BLOCK_W = 16
K_BLOCK_W = KW + Q_BLOCK_W
D_FF = -(-8 * D_MODEL // (3 * 256)) * 256
D_IN = 3 * D_NA + D_F
EPS = 1e-6

kernel_name = "hybrid_natten_fnet_diffusion_step"


def rms_norm(x, g):
    xf = x.astype(jnp.float32)
    y = xf * lax.rsqrt(jnp.mean(xf * xf, axis=-1, keepdims=True) + EPS)
    return (y * g.astype(jnp.float32)).astype(x.dtype)


def modulation(cond, w_mod, b_mod):
    m = jax.nn.silu(cond) @ w_mod + b_mod
    return jnp.split(m[..., None, :], 6, axis=-1)


def project_in(h, w_in, q_g, k_g):
    p = h @ w_in
    q, k, v, u = jnp.split(p, [D_NA, 2 * D_NA, 3 * D_NA], axis=-1)
    shp = h.shape[:-1] + (N_HEADS, HEAD_DIM)
    q = rms_norm(q.reshape(shp), q_g)
    k = rms_norm(k.reshape(shp), k_g)
    return q, k, v.reshape(shp), u


def fourier_mix(u):
    b, t, _ = u.shape
    ug = u.reshape(b, t, N_FGROUPS, F_GROUP).astype(jnp.float32)
    f = jnp.fft.fft2(ug, axes=(1, 3), norm="ortho").real
    return f.reshape(b, t, D_F).astype(u.dtype)


def context_attention(q, k, v):
    s = jnp.einsum('bqhd,bkhd->bhqk', q, k).astype(jnp.float32) * (HEAD_DIM ** -0.5)
    p = jax.nn.softmax(s, axis=-1).astype(v.dtype)
    return jnp.einsum('bhqk,bkhd->bqhd', p, v)


def neighbourhood_attention(q, k, v, k_ctx, v_ctx, rpb):
    b, t, h, d = q.shape
    rows = t // GRID_W
    kh = min(KH_MAX, rows)
    nqb = GRID_W // Q_BLOCK_W
    r = jnp.arange(rows)
    row_idx = jnp.clip(r - kh // 2, 0, rows - kh)[:, None] + jnp.arange(kh)
    c0 = jnp.arange(nqb) * Q_BLOCK_W
    col_idx = jnp.clip(c0 - KW // 2, 0, GRID_W - K_BLOCK_W)[:, None] + jnp.arange(K_BLOCK_W)
    qc = c0[:, None] + jnp.arange(Q_BLOCK_W)
    win_start = jnp.clip(qc - KW // 2, 0, GRID_W - KW)[..., None]
    kc = col_idx[:, None, :]
    in_win = (kc >= win_start) & (kc < win_start + KW)
    dr = row_idx - r[:, None] + (KH_MAX - 1)
    dc = jnp.clip(kc - qc[..., None] + (KW - 1), 0, 2 * KW - 2)
    bias = rpb[:, dr[:, None, None, :, None], dc[None, :, :, None, :]].astype(jnp.float32)
    bias = jnp.where(in_win[None, None, :, :, None, :], bias, -jnp.inf)
    nk = kh * K_BLOCK_W
    bias = bias.reshape(h, rows, nqb, Q_BLOCK_W, nk)

    qg = q.reshape(b, rows, nqb, Q_BLOCK_W, h, d)
    kg = k.reshape(b, rows, GRID_W, h, d)
    vg = v.reshape(b, rows, GRID_W, h, d)
    ri = row_idx[:, None, :, None]
    ci = col_idx[None, :, None, :]
    kb = kg[:, ri, ci].reshape(b, rows, nqb, nk, h, d)
    vb = vg[:, ri, ci].reshape(b, rows, nqb, nk, h, d)

    scale = HEAD_DIM ** -0.5
    s_loc = jnp.einsum('brnqhd,brnkhd->bhrnqk', qg, kb).astype(jnp.float32) * scale + bias
    s_ctx = jnp.einsum('brnqhd,blhd->bhrnql', qg, k_ctx).astype(jnp.float32) * scale
    p = jax.nn.softmax(jnp.concatenate([s_loc, s_ctx], axis=-1), axis=-1).astype(v.dtype)
    o = (jnp.einsum('bhrnqk,brnkhd->brnqhd', p[..., :nk], vb)
         + jnp.einsum('bhrnql,blhd->brnqhd', p[..., nk:], v_ctx))
    return o.reshape(b, t, h, d)


def token_mixing(h, attn, u, w_na_proj, w_fnet_proj, w_gate, w_o):
    b, t = h.shape[:2]
    na = attn.reshape(b, t, D_NA) @ w_na_proj
    fn = fourier_mix(u) @ w_fnet_proj
    g_na, g_fn = jnp.split(jax.nn.sigmoid(h @ w_gate), 2, axis=-1)
    return (g_na * na + g_fn * fn) @ w_o


def swiglu(h, w_gate_up, w_down):
    a, g = jnp.split(h @ w_gate_up, 2, axis=-1)
    return (jax.nn.silu(g) * a) @ w_down


def trunk_layer(x, cond, lw, attend):
    (w_mod_l, b_mod_l, n1, n2, w_in_l, qg_l, kg_l,
     w_na_proj_l, w_fnet_proj_l, w_gate_l, w_o_l, w_gu_l, w_down_l) = lw
    sh1, sc1, g1, sh2, sc2, g2 = modulation(cond, w_mod_l, b_mod_l)
    h = rms_norm(x, n1) * (1 + sc1) + sh1
    q, k, v, u = project_in(h, w_in_l, qg_l, kg_l)
    attn = attend(q, k, v)
    x = x + g1 * token_mixing(h, attn, u, w_na_proj_l, w_fnet_proj_l, w_gate_l, w_o_l)
    h2 = rms_norm(x, n2) * (1 + sc2) + sh2
    x = x + g2 * swiglu(h2, w_gu_l, w_down_l)
    return x, k, v


def setup_inputs(seed: int = 0) -> dict:
    key = jax.random.key(seed)
    ks = jax.random.split(key, 24)
    nrm = lambda k, shape, s: jax.random.normal(k, shape, jnp.float32) * s
    return {
        "x_prompt": nrm(ks[0], (BATCH, SEQ, D_MODEL), 1.0),
        "x_sample": nrm(ks[1], (DEC_BATCH, DEC_SEQ, D_MODEL), 1.0),
        "cache_k": nrm(ks[2], (DEC_BATCH, DEPTH, PAST_LEN, N_HEADS, HEAD_DIM), 1.0),
        "cache_v": nrm(ks[3], (DEC_BATCH, DEPTH, PAST_LEN, N_HEADS, HEAD_DIM), 1.0),
        "c": nrm(ks[4], (DEC_BATCH, D_MODEL), 1.0),
        "c_ctx": nrm(ks[5], (D_MODEL,), 1.0),
        "w_mod": nrm(ks[6], (DEPTH, D_MODEL, 6 * D_MODEL), 0.5 * D_MODEL ** -0.5),
        "b_mod": nrm(ks[7], (DEPTH, 6 * D_MODEL), 0.02),
        "norm1_g": 1.0 + nrm(ks[8], (DEPTH, D_MODEL), 0.01),
        "norm2_g": 1.0 + nrm(ks[9], (DEPTH, D_MODEL), 0.01),
        "w_in": nrm(ks[10], (DEPTH, D_MODEL, D_IN), D_MODEL ** -0.5),
        "q_norm_g": 1.0 + nrm(ks[11], (DEPTH, HEAD_DIM), 0.01),
        "k_norm_g": 1.0 + nrm(ks[12], (DEPTH, HEAD_DIM), 0.01),
        "rpb": nrm(ks[13], (DEPTH, N_HEADS, 2 * KH_MAX - 1, 2 * KW - 1), 0.1),
        "w_na_proj": nrm(ks[14], (DEPTH, D_NA, D_MODEL), D_NA ** -0.5),
        "w_fnet_proj": nrm(ks[15], (DEPTH, D_F, D_MODEL), D_F ** -0.5),
        "w_gate": nrm(ks[16], (DEPTH, D_MODEL, 2 * D_MODEL), D_MODEL ** -0.5),
        "w_o": nrm(ks[17], (DEPTH, D_MODEL, D_MODEL), D_MODEL ** -0.5),
        "w_gate_up": nrm(ks[18], (DEPTH, D_MODEL, 2 * D_FF), D_MODEL ** -0.5),
        "w_down": nrm(ks[19], (DEPTH, D_FF, D_MODEL), D_FF ** -0.5),
    }


def reference(x_prompt, x_sample, cache_k, cache_v, c, c_ctx, w_mod, b_mod, norm1_g, norm2_g,
              w_in, q_norm_g, k_norm_g, rpb, w_na_proj, w_fnet_proj, w_gate, w_o, w_gate_up, w_down):
    y_prompt = x_prompt
    y_sample = x_sample
    new_ks = []
    new_vs = []
    for l in range(DEPTH):
        lw = (w_mod[l], b_mod[l], norm1_g[l], norm2_g[l], w_in[l], q_norm_g[l], k_norm_g[l],
              w_na_proj[l], w_fnet_proj[l], w_gate[l], w_o[l], w_gate_up[l], w_down[l])
        y_prompt, k_ctx, v_ctx = trunk_layer(y_prompt, c_ctx, lw, context_attention)
        new_ks.append(k_ctx)
        new_vs.append(v_ctx)
        ck = cache_k[:, l]
        cv = cache_v[:, l]
        rpb_l = rpb[l]
        attend_lat = lambda q, k, v, ck=ck, cv=cv, rpb_l=rpb_l: neighbourhood_attention(q, k, v, ck, cv, rpb_l)
        y_sample, _, _ = trunk_layer(y_sample, c, lw, attend_lat)
    new_k = jnp.stack(new_ks, axis=1)
    new_v = jnp.stack(new_vs, axis=1)
    return (y_prompt, y_sample, new_k, new_v)
```

```python
import numpy as np
from contextlib import ExitStack
import concourse.bass as bass
import concourse.mybir as mybir
from concourse.bass_utils import run_bass_kernel_spmd

F32 = mybir.dt.float32
BF16 = mybir.dt.bfloat16
ALU = mybir.AluOpType
AF = mybir.ActivationFunctionType
AX = mybir.AxisListType

D = 2048
T = 1024
NDC = 16
DFF = 5632
NDS = 12
NSLOT = 2
SLOTW = 4096
EPS = 1e-6
NEGB = -64.0


class Buf:
    __slots__ = ("w", "r", "name")

    def __init__(self, name, init=()):
        self.name = name
        self.w = list(init)
        self.r = []


class Rec:
    def __init__(self, name):
        self.name = name
        self.ops = []
        self.count = 0
        self.seen = {}
        self.dk = 0
        self.dlast = [None] * NDS
        self.dval = [0] * NDS


class Prog:
    def __init__(self):
        self.eng = {n: Rec(n) for n in ("pe", "act", "dve", "pool", "sp")}
        self.epoch = []

    def need(self, e, tok):
        key, val = tok
        if key == ("e", "pe") and e.name == "pe":
            return
        if e.seen.get(key, 0) >= val:
            return
        e.seen[key] = val
        e.ops.append(("wait", key, val))

    def _deps(self, e, reads, writes):
        for b in reads:
            for t in b.w:
                self.need(e, t)
        for b in writes:
            for t in b.w:
                self.need(e, t)
            for t in b.r:
                self.need(e, t)

    @staticmethod
    def _addr(b, tok):
        for i, t in enumerate(b.r):
            if t[0] == tok[0]:
                if t[1] < tok[1]:
                    b.r[i] = tok
                return
        b.r.append(tok)

    def _upd(self, tok, reads, writes):
        for b in reads:
            self._addr(b, tok)
        for b in writes:
            b.w = [tok]
            b.r = []

    def op(self, en, fn, reads=(), writes=()):
        e = self.eng[en]
        self._deps(e, reads, writes)
        e.count += 1
        tok = (("e", en), e.count)
        e.ops.append(("op", fn, True))
        self._upd(tok, reads, writes)
        return tok

    def pe_group(self, fns, reads=(), writes=()):
        e = self.eng["pe"]
        self._deps(e, reads, writes)
        for fn in fns[:-1]:
            e.ops.append(("op", fn, False))
        e.count += 1
        tok = (("e", "pe"), e.count)
        e.ops.append(("op", fns[-1], True))
        self._upd(tok, reads, writes)
        return tok

    def dma(self, q, out, in_, reads=(), writes=()):
        e = self.eng[q]
        k = e.dk % NDS
        e.dk += 1
        key = ("d", q, k)
        if e.dlast[k] is not None:
            self.need(e, e.dlast[k])
        self._deps(e, reads, writes)
        e.dval[k] += 16
        tok = (key, e.dval[k])
        e.dlast[k] = tok
        e.ops.append(("dma", out, in_, key))
        self._upd(tok, reads, writes)
        return tok

    def dma_add(self, q, out, in_, group, reads=()):
        tok = self.dma(q, out, in_, reads=reads, writes=[Buf("_g")])
        group.w.append(tok)
        return tok

    def barrier(self):
        toks = []
        for n in ("pe", "act", "dve"):
            e = self.eng[n]
            if e.count:
                toks.append((("e", n), e.count))
        for q in ("pool", "sp"):
            e = self.eng[q]
            for t in e.dlast:
                if t is not None:
                    toks.append(t)
        self.epoch = toks

    def buf(self, name):
        return Buf(name, self.epoch)


def build_program(depth=2, stop=None):
    nc = bass.Bass("TRN2", target_bir_lowering=False)
    P = Prog()

    def din(name, shape):
        return nc.dram_tensor(name, list(shape), F32, kind="ExternalInput").ap()

    def dout(name, shape):
        return nc.dram_tensor(name, list(shape), F32, kind="ExternalOutput").ap()

    x_d = din("x", [T, D])
    cond_d = din("cond", [128, 16])
    bmod_d = din("bmod", [2, 128, 96])
    n1g_d = din("n1g", [2, 128, 16])
    n2g_d = din("n2g", [2, 128, 16])
    qg_d = din("qg", [2, 128, 64])
    kg_d = din("kg", [2, 128, 64])
    wmod_d = din("w_mod", [2, D, 6 * D])
    win_d = din("w_in", [2, D, 4096])
    wna_d = din("w_na", [2, 1024, D])
    wfn_d = din("w_fn", [2, 1024, D])
    wgate_d = din("w_gate", [2, D, 4096])
    wo_d = din("w_o", [2, D, D])
    wgu_d = din("w_gu", [2, D, 2 * DFF])
    wdn_d = din("w_dn", [2, DFF, D])
    ck_d = din("ck", [2, 256, 1024])
    cv_d = din("cv", [2, 256, 1024])
    tab_d = din("tab", [2, 16, 64, 15 * 64])
    qx_d = din("qx", [17, 1024])
    kx_d = din("kx", [17, 1280])
    ct_d = din("ct", [1024, 1024])
    nst_d = din("nst", [1024, 1024])
    cs_d = din("cs", [256, 512])
    idf_d = din("identf", [128, 128])
    id8_d = din("ident8", [128, 128])
    oy_d = dout("oy", [T, D])
    ok_d = dout("ok", [2, T, 1024])
    ov_d = dout("ov", [2, T, 1024])

    es = ExitStack()
    AW = 53080
    A = es.enter_context(nc.sbuf_tensor("arena", [128, AW], F32))
    psS = [es.enter_context(nc.psum_tensor(f"psS{i}", [128, 512], F32)) for i in range(4)]
    psO = [es.enter_context(nc.psum_tensor(f"psO{i}", [128, 512], F32)) for i in range(2)]
    psT = [es.enter_context(nc.psum_tensor(f"psT{i}", [128, 1024], BF16)) for i in range(2)]

    def fv(off, n):
        return A[:, off:off + n]

    def bv(off, nwords):
        return A[:, off:off + nwords].bitcast(BF16)

    o = 0
    XT = fv(o, 16384).rearrange("p (c t) -> p c t", c=16); o += 16384
    slots = []
    for i in range(NSLOT):
        slots.append(bv(o, SLOTW)); o += SLOTW
    mslots = []
    for i in range(2):
        mslots.append(bv(o, 1024)); o += 1024
    ident_bf = bv(o, 64); o += 64
    ident8 = bv(o, 64); o += 64
    ones_bf = bv(o, 64); o += 64
    identf = fv(o, 128); o += 128
    mods = []
    for l in range(2):
        mods.append(fv(o, 96)); o += 96
    bmods = []
    for l in range(2):
        bmods.append(fv(o, 96)); o += 96
    gm = [[None, None], [None, None]]
    for l in range(2):
        for i in range(2):
            gm[l][i] = fv(o, 16); o += 16
    ng = [[None, None], [None, None]]
    for l in range(2):
        for i in range(2):
            ng[l][i] = fv(o, 16); o += 16
    qgb = []
    kgb = []
    for l in range(2):
        qgb.append(fv(o, 64)); o += 64
        kgb.append(fv(o, 64)); o += 64
    cond_s = fv(o, 16); o += 16
    s_bf = bv(o, 8); o += 8
    rstd = fv(o, 1024); o += 1024
    R0 = o
    assert R0 <= 28760, R0
    R0 = 28760
    RW = AW - R0

    b_xT = [Buf(f"xT{c}") for c in range(16)]
    b_slot = [Buf(f"slot{i}") for i in range(NSLOT)]
    b_mslot = [Buf(f"mslot{i}") for i in range(2)]
    b_psS = [Buf(f"psS{i}") for i in range(4)]
    b_psO = [Buf(f"psO{i}") for i in range(2)]
    b_psT = [Buf(f"psT{i}") for i in range(2)]
    b_const = Buf("const")
    b_mod = [Buf("mod0"), Buf("mod1")]
    b_rstd = Buf("rstd")
    b_ok = [Buf("ok0"), Buf("ok1")]
    b_ov = [Buf("ov0"), Buf("ov1")]
    state = {"slot": 0, "psS": 0, "psT": 0, "ev": 0, "mslot": 0}

    def next_psS():
        i = state["psS"] % 4
        state["psS"] += 1
        return b_psS[i], psS[i]

    def next_psT():
        i = state["psT"] % 2
        state["psT"] += 1
        return b_psT[i], psT[i]

    def wload(dram_ap, pattern, nelem=2 * SLOTW, **kw):
        i = state["slot"] % NSLOT
        state["slot"] += 1
        view = slots[i][:, 0:nelem].rearrange(pattern, **kw)
        if isinstance(dram_ap, (list, tuple)):
            for si, dap in enumerate(dram_ap):
                P.dma("pool", view[:, :, si, :], dap, writes=[b_slot[i]])
        else:
            P.dma("pool", view, dram_ap, writes=[b_slot[i]])
        return b_slot[i], view

    def evac_engine():
        state["ev"] += 1
        return "act" if state["ev"] % 2 else "dve"

    def copy_op(en, out, in_, reads, writes):
        if en == "act":
            return P.op("act", lambda h: h.activation(out=out, in_=in_, func=AF.Copy), reads, writes)
        return P.op("dve", lambda h: h.tensor_copy(out=out, in_=in_), reads, writes)

    def mmf(out, lhsT, rhs, start, stop):
        return lambda h: h.matmul(out, lhsT, rhs, start=start, stop=stop)

    def trf(out, in_, ident):
        return lambda h: h.transpose(out, in_, ident)

    P.op("dve", lambda h: h.memset(ones_bf, 1.0), writes=[b_const])
    P.dma_add("sp", cond_s, cond_d, b_const)
    P.dma_add("sp", identf, idf_d, b_const)
    P.dma_add("pool", ident_bf, idf_d, b_const)
    P.dma_add("pool", ident8, id8_d, b_const)
    for l in range(2):
        P.dma_add("sp", bmods[l], bmod_d[l], b_const)
        P.dma_add("sp", ng[l][0], n1g_d[l], b_const)
        P.dma_add("sp", ng[l][1], n2g_d[l], b_const)
        P.dma_add("sp", qgb[l], qg_d[l], b_const)
        P.dma_add("sp", kgb[l], kg_d[l], b_const)
    P.op("act", lambda h: h.activation(out=s_bf, in_=cond_s, func=AF.Silu), reads=[b_const], writes=[b_const])

    P.barrier()
    xin = fv(R0, 16384).rearrange("p (t d) -> p t d", t=8)
    b_xin = [P.buf(f"xin{t}") for t in range(8)]
    xv = x_d.rearrange("(t p) d -> p t d", p=128)
    for t in range(8):
        P.dma("sp", xin[:, t, :], xv[:, t, :], writes=[b_xin[t]])

    mod_q = []
    for l_ in range(depth):
        for w_ in range(6):
            for b_ in range(16):
                mod_q.append((l_, w_, b_))
    mod_done = set()
    mod_state = {"in_attn": False, "hb": 0}
    b_mq = [[Buf(f"mod{l_}_{w_}") for w_ in range(6)] for l_ in range(2)]

    def mod_block():
        l_, w_, b_ = mod_q.pop(0)
        bp, ps = b_psO[0], psO[0]
        col0 = w_ * 2048 + b_ * 128
        i = state["mslot"] % 2
        state["mslot"] += 1
        sv = mslots[i].rearrange("p (k n) -> p k n", k=16)
        bs = b_mslot[i]
        P.dma("pool", sv, wmod_d[l_].rearrange("(kc p) n -> p kc n", p=128)[:, :, col0:col0 + 128], writes=[bs])
        fns = [mmf(ps[:, b_:b_ + 1], sv[:, kc, :], s_bf[:, kc:kc + 1], kc == 0, kc == 15) for kc in range(16)]
        P.pe_group(fns, reads=[bs, b_const], writes=[bp])
        if b_ == 15:
            mod_fin(l_, w_)

    def mod_fin(l_, w_):
        bp, ps = b_psO[0], psO[0]
        md = mods[l_]
        bm = bmods[l_]
        P.op("dve", lambda h: h.tensor_tensor(out=md[:, w_ * 16:(w_ + 1) * 16], in0=ps[:, 0:16], in1=bm[:, w_ * 16:(w_ + 1) * 16], op=ALU.add),
             reads=[bp, b_const], writes=[b_mq[l_][w_]])
        if w_ == 1:
            P.op("dve", lambda h: h.scalar_tensor_tensor(out=gm[l_][0], in0=md[:, 16:32], scalar=1.0, in1=ng[l_][0],
                                                          op0=ALU.add, op1=ALU.mult), reads=[b_const], writes=[b_mq[l_][1]])
        if w_ == 4:
            P.op("dve", lambda h: h.scalar_tensor_tensor(out=gm[l_][1], in0=md[:, 64:80], scalar=1.0, in1=ng[l_][1],
                                                          op0=ALU.add, op1=ALU.mult), reads=[b_const], writes=[b_mq[l_][4]])
        mod_done.add((l_, w_))

    def mod_unit_big(l_, w_):
        bp, ps = b_psO[0], psO[0]
        for b4 in range(4):
            col0 = w_ * 2048 + b4 * 512
            bs, sv = wload(wmod_d[l_].rearrange("(kc p) n -> p kc n", p=128)[:, :, col0:col0 + 512], "p (k n) -> p k n", k=16)
            fns = []
            for j in range(4):
                cc = b4 * 4 + j
                for kc in range(16):
                    fns.append(mmf(ps[:, cc:cc + 1], sv[:, kc, j * 128:(j + 1) * 128], s_bf[:, kc:kc + 1], kc == 0, kc == 15))
            P.pe_group(fns, reads=[bs, b_const], writes=[bp])
        mod_fin(l_, w_)
        mod_q[:] = [m for m in mod_q if not (m[0] == l_ and m[1] == w_)]

    def mod_require(l_, w_):
        while (l_, w_) not in mod_done:
            mod_block()

    def mod_finish_unit():
        while mod_q and mod_q[0][2] != 0:
            mod_block()

    def heavy_done():
        mod_state["hb"] += 1
        for _ in range(2):
            if mod_q and not mod_state["in_attn"]:
                mod_block()

    def mod_idle(n):
        return

    mod_unit_big(0, 0)
    mod_unit_big(0, 1)
    for th in range(2):
        for dc in range(16):
            bp, ps = next_psS()
            fns = [trf(ps[:, j * 128:(j + 1) * 128], xin[:, th * 4 + j, dc * 128:(dc + 1) * 128], identf) for j in range(4)]
            P.pe_group(fns, reads=[b_const] + b_xin[th * 4:th * 4 + 4], writes=[bp])
            copy_op(evac_engine(), XT[:, dc, th * 512:(th + 1) * 512], ps[:, :], [bp], [b_xT[dc]])

    O_HT = R0
    O_QN = R0 + 8192
    O_UT = R0 + 12288
    O_OT = R0 + 16384
    O_SC = R0 + 20480

    def norm_stats(l, sc_off):
        sq = [bv(sc_off + i * 256, 256) for i in range(3)]
        b_sq = [P.buf(f"sq{i}") for i in range(3)]
        k = 0
        for th in range(2):
            bp, ps = next_psS()
            for dc in range(16):
                i = k % 3
                k += 1
                sqi = sq[i]
                if dc % 8 in (0, 3, 6):
                    P.op("act", lambda h, sqi=sqi, dc=dc, th=th: h.activation(out=sqi, in_=XT[:, dc, th * 512:(th + 1) * 512], func=AF.Square),
                         reads=[b_xT[dc]], writes=[b_sq[i]])
                else:
                    P.op("dve", lambda h, sqi=sqi, dc=dc, th=th: h.tensor_tensor(out=sqi, in0=XT[:, dc, th * 512:(th + 1) * 512],
                                                                               in1=XT[:, dc, th * 512:(th + 1) * 512], op=ALU.mult),
                         reads=[b_xT[dc]], writes=[b_sq[i]])
                P.pe_group([mmf(ps[:, :], ones_bf, sqi, dc == 0, dc == 15)], reads=[b_sq[i], b_const], writes=[bp])
            rs = rstd[:, th * 512:(th + 1) * 512]
            P.op("act", lambda h, rs=rs, ps=ps: h.activation(out=rs, in_=ps[:, :], func=AF.Sqrt, bias=EPS, scale=1.0 / D),
                 reads=[bp], writes=[b_rstd])
            P.op("dve", lambda h, rs=rs: h.reciprocal(out=rs, in_=rs), reads=[b_rstd], writes=[b_rstd])

    def make_h(l, which, hT, b_hT, sc_off):
        g = gm[l][which]
        bmods_ = [b_mq[l][0], b_mq[l][1]] if which == 0 else [b_mq[l][3], b_mq[l][4]]
        sh = mods[l][:, (0 if which == 0 else 48):(16 if which == 0 else 64)]
        tmp = [fv(sc_off + i * 512, 512) for i in range(3)]
        b_tmp = [P.buf(f"tmp{i}") for i in range(3)]
        k = 0
        for dc in range(16):
            for th in range(2):
                i = k % 3
                k += 1
                tm = tmp[i]
                P.op("dve", lambda h, tm=tm, dc=dc, th=th: h.scalar_tensor_tensor(
                    out=tm, in0=XT[:, dc, th * 512:(th + 1) * 512], scalar=g[:, dc:dc + 1],
                    in1=rstd[:, th * 512:(th + 1) * 512], op0=ALU.mult, op1=ALU.mult),
                    reads=[b_xT[dc], b_rstd] + bmods_, writes=[b_tmp[i]])
                P.op("act", lambda h, tm=tm, dc=dc, th=th: h.activation(
                    out=hT[:, dc, th * 512:(th + 1) * 512], in_=tm, func=AF.Identity, bias=sh[:, dc:dc + 1], scale=1.0),
                    reads=[b_tmp[i]] + bmods_, writes=[b_hT[dc]])

    def layer(l):
        mod_require(l, 0)
        mod_require(l, 1)
        P.barrier()
        hT = bv(O_HT, 8192).rearrange("p (c t) -> p c t", c=16)
        b_hT = [P.buf(f"hT{c}") for c in range(16)]
        norm_stats(l, O_SC)
        mod_idle(4)
        make_h(l, 0, hT, b_hT, O_SC + 768)
        P.barrier()
        qn = bv(O_QN, 4096).rearrange("p (t f) -> p t f", t=8)
        b_qn = [P.buf(f"qn{t}") for t in range(8)]
        UT = bv(O_UT, 4096).rearrange("p (c t) -> p c t", c=8)
        b_UT = [P.buf(f"UT{c}") for c in range(8)]
        OT = bv(O_OT, 4096).rearrange("p (c t) -> p c t", c=8)
        b_OT = [P.buf(f"OT{c}") for c in range(8)]
        sc2 = O_SC
        sqf_ = [fv(sc2, 512), fv(sc2 + 512, 512)]
        tq_ = [fv(sc2 + 1024, 512), fv(sc2 + 1536, 512)]
        st_ = [fv(sc2 + 2048, 8), fv(sc2 + 2072, 8)]
        st2_ = [fv(sc2 + 2056, 8), fv(sc2 + 2080, 8)]
        st3_ = [fv(sc2 + 2064, 8), fv(sc2 + 2088, 8)]
        kst = [fv(sc2 + 2096 + i * 512, 512) for i in range(3)]
        b_sqf_ = [P.buf("sqf0"), P.buf("sqf1")]; b_tq_ = [P.buf("tq0"), P.buf("tq1")]; b_st_ = [P.buf("st0"), P.buf("st1")]
        nrm_i = [0]
        b_kst = [P.buf(f"kst{i}") for i in range(3)]
        kcount = [0]
        okv = ok_d[l].rearrange("(t p) f -> p t f", p=128)
        ovv = ov_d[l].rearrange("(t p) f -> p t f", p=128)
        winv = win_d[l].rearrange("(kc p) n -> p kc n", p=128)
        for part in range(3):
            for cb in range(2):
                col0 = part * 1024 + cb * 512
                bs, sv = wload(winv[:, :, col0:col0 + 512], "p (k n) -> p k n", k=16)
                for tt in range(8):
                    bp, ps = next_psS()
                    fns = [mmf(ps[:, :], hT[:, kc, tt * 128:(tt + 1) * 128], sv[:, kc, :], kc == 0, kc == 15) for kc in range(16)]
                    P.pe_group(fns, reads=[bs] + b_hT, writes=[bp])
                    pv = ps[:, :]
                    if part < 2:
                        ni = nrm_i[0] % 2
                        nrm_i[0] += 1
                        sqf = sqf_[ni]; tq = tq_[ni]; st = st_[ni]; st2 = st2_[ni]; st3 = st3_[ni]
                        b_sqf = b_sqf_[ni]; b_tq = b_tq_[ni]; b_st = b_st_[ni]
                        gbc = (qgb[l] if part == 0 else kgb[l]).unsqueeze(1).to_broadcast([128, 8, 64])
                        P.op("act", lambda h, pv=pv, sqf=sqf: h.activation(out=sqf, in_=pv, func=AF.Square), reads=[bp], writes=[b_sqf])
                        P.op("dve", lambda h, st=st, sqf=sqf: h.tensor_reduce(out=st, in_=sqf.rearrange("p (a b) -> p a b", a=8), axis=AX.X, op=ALU.add),
                             reads=[b_sqf], writes=[b_st])
                        P.op("act", lambda h, st=st, st2=st2: h.activation(out=st2, in_=st, func=AF.Sqrt, bias=EPS, scale=1.0 / 64), reads=[b_st], writes=[b_st])
                        P.op("dve", lambda h, st2=st2, st3=st3: h.reciprocal(out=st3, in_=st2), reads=[b_st], writes=[b_st])
                        P.op("dve", lambda h, pv=pv, tq=tq, st3=st3: h.tensor_tensor(out=tq.rearrange("p (a b) -> p a b", a=8),
                                                                     in0=pv.rearrange("p (a b) -> p a b", a=8),
                                                                     in1=st3.unsqueeze(2).to_broadcast([128, 8, 64]), op=ALU.mult),
                             reads=[bp, b_st], writes=[b_tq])
                        if part == 0:
                            P.op("dve", lambda h, tt=tt, cb=cb, gbc=gbc, tq=tq: h.tensor_tensor(
                                out=qn[:, tt, cb * 512:(cb + 1) * 512].rearrange("p (a b) -> p a b", a=8),
                                in0=tq.rearrange("p (a b) -> p a b", a=8), in1=gbc, op=ALU.mult),
                                reads=[b_tq, b_const], writes=[b_qn[tt]])
                        else:
                            i = kcount[0] % 3
                            kcount[0] += 1
                            ks = kst[i]
                            P.op("dve", lambda h, ks=ks, gbc=gbc, tq=tq: h.tensor_tensor(
                                out=ks.rearrange("p (a b) -> p a b", a=8),
                                in0=tq.rearrange("p (a b) -> p a b", a=8), in1=gbc, op=ALU.mult),
                                reads=[b_tq, b_const], writes=[b_kst[i]])
                            P.dma_add("sp", okv[:, tt, cb * 512:(cb + 1) * 512], ks, b_ok[l], reads=[b_kst[i]])
                    else:
                        i = kcount[0] % 3
                        kcount[0] += 1
                        ks = kst[i]
                        copy_op(evac_engine(), ks, pv, [bp], [b_kst[i]])
                        P.dma_add("sp", ovv[:, tt, cb * 512:(cb + 1) * 512], ks, b_ov[l], reads=[b_kst[i]])
                heavy_done()
        if stop == 'p2' and l == 1:
            return
        for cb in range(2):
            col0 = 3072 + cb * 512
            bs, sv = wload(winv[:, :, col0:col0 + 512], "p (k n) -> p k n", k=16)
            for j in range(4):
                uc = cb * 4 + j
                for th in range(2):
                    bp, ps = next_psS()
                    fns = [mmf(ps[:, :], sv[:, kc, j * 128:(j + 1) * 128], hT[:, kc, th * 512:(th + 1) * 512], kc == 0, kc == 15) for kc in range(16)]
                    P.pe_group(fns, reads=[bs] + b_hT, writes=[bp])
                    copy_op(evac_engine(), UT[:, uc, th * 512:(th + 1) * 512], ps[:, :], [bp], [b_UT[uc]])
            heavy_done()
        P.barrier()
        CT = bv(O_HT, 4096).rearrange("p (t n) -> p t n", t=8)
        NST = bv(O_HT + 4096, 4096).rearrange("p (t n) -> p t n", t=8)
        b_ct = P.buf("dft_ct"); b_nst = P.buf("dft_nst"); b_cs = P.buf("dft_cs")
        AB = bv(O_SC, 2048).rearrange("p (t n) -> p t n", t=8)
        CS = bv(O_SC + 2048, 512).rearrange("p (c n) -> p c n", c=2)
        b_AB = [P.buf(f"AB{t}") for t in range(8)]
        P.dma("pool", CS, cs_d.rearrange("(c p) n -> p c n", p=128), writes=[b_cs])
        P.dma("pool", CT, ct_d.rearrange("(t p) n -> p t n", p=128), writes=[b_ct])
        P.dma("pool", NST, nst_d.rearrange("(t p) n -> p t n", p=128), writes=[b_nst])
        mod_idle(2)
        for g in range(4):
            if g > 0:
                mod_idle(1)
            for tt in range(8):
                bp, ps = next_psS()
                fns = [mmf(ps[:, :], UT[:, 2 * g + c2, tt * 128:(tt + 1) * 128], CS[:, c2, :], c2 == 0, c2 == 1) for c2 in range(2)]
                P.pe_group(fns, reads=[b_cs, b_UT[2 * g], b_UT[2 * g + 1]], writes=[bp])
                copy_op(evac_engine(), AB[:, tt, :], ps[:, :], [bp], [b_AB[tt]])
            for j in range(2):
                for th in range(2):
                    bp, ps = next_psS()
                    fns = []
                    for tt in range(8):
                        fns.append(mmf(ps[:, :], AB[:, tt, j * 128:(j + 1) * 128], CT[:, tt, th * 512:(th + 1) * 512], tt == 0, False))
                        fns.append(mmf(ps[:, :], AB[:, tt, 256 + j * 128:256 + (j + 1) * 128], NST[:, tt, th * 512:(th + 1) * 512], False, tt == 7))
                    P.pe_group(fns, reads=[b_ct, b_nst] + b_AB, writes=[bp])
                    copy_op(evac_engine(), UT[:, 2 * g + j, th * 512:(th + 1) * 512], ps[:, :], [bp], [b_UT[2 * g + j]])
        FT = UT
        b_FT = b_UT
        if stop == 'p5' and l == 1:
            return
        mod_finish_unit()
        mod_state["in_attn"] = True
        P.barrier()
        o3 = O_HT
        KV = []
        vflat = []
        for i in range(2):
            kp = bv(o3, 640).rearrange("p (c f) -> p c f", c=10); o3 += 640
            vflat.append(bv(o3, 1280))
            vp = bv(o3, 1280).rearrange("p (c h f) -> p c h f", c=10, h=2); o3 += 1280
            KV.append((kp, vp))
        b_K = [P.buf("k0"), P.buf("k1")]
        b_Kc = [P.buf("kc0"), P.buf("kc1")]
        b_V = [[P.buf(f"v{i}{hh}") for hh in range(2)] for i in range(2)]
        b_Vc = [[P.buf(f"vc{i}{hh}") for hh in range(2)] for i in range(2)]
        qTa = []
        kTa = []
        for i in range(2):
            qTa.append(bv(o3, 512)); o3 += 512
            kTa.append(bv(o3, 640)); o3 += 640
        b_qTa = [P.buf("qTa0"), P.buf("qTa1")]
        b_kTa = [P.buf("kTa0"), P.buf("kTa1")]
        b_kTa_c = [P.buf("kTac0"), P.buf("kTac1")]
        b_qx = [P.buf("qx0"), P.buf("qx1")]
        b_kx = [P.buf("kx0"), P.buf("kx1")]
        NPT = 6
        PT = []
        for i in range(NPT):
            PT.append(bv(o3, 256)); o3 += 256
        b_PT = [P.buf(f"PT{i}") for i in range(NPT)]
        rd = fv(o3, 512); o3 += 512
        b_rd = P.buf("rd")
        assert o3 <= O_HT + 8192, o3 - O_HT
        tabb = [bv(O_SC, 960), bv(O_SC + 960, 960)]
        b_tab = [P.buf("tab0"), P.buf("tab1")]
        b_tab2 = [P.buf("tab0b"), P.buf("tab1b")]
        for i in range(2):
            P.op("dve", lambda h, i=i: h.memset(tabb[i], 0.0), writes=[b_tab[i], b_tab2[i]])
            P.op("dve", lambda h, i=i: h.memset(vflat[i], 1.0), writes=[b_V[i][0], b_V[i][1], b_Vc[i][0], b_Vc[i][1]])
            P.dma("pool", qTa[i][64:81, :], qx_d, writes=[b_qx[i]])
            P.dma("pool", kTa[i][64:81, :], kx_d, writes=[b_kx[i]])
        ckv = ck_d[l].rearrange("(c p) f -> p c f", p=128)
        cvv = cv_d[l].rearrange("(c p) f -> p c f", p=128)
        ptc = [0]
        occ = [0]
        KCS = {0: [0, 1, 2, 3, 4, 5, 8, 9], 1: [2, 3, 4, 5, 6, 7, 8, 9]}

        def load_pair(j):
            kp, vp = KV[j % 2]
            i = j % 2
            P.dma("pool", kp[:, 0:8, :], okv[:, :, j * 128:(j + 1) * 128], reads=[b_ok[l]], writes=[b_K[i]])
            P.dma("pool", kp[:, 8:10, :], ckv[:, :, j * 128:(j + 1) * 128], writes=[b_Kc[i]])
            for hh in range(2):
                c0 = j * 128 + hh * 64
                P.dma("pool", vp[:, 0:8, hh, 0:64], ovv[:, :, c0:c0 + 64], reads=[b_ov[l]], writes=[b_V[i][hh]])
                P.dma("pool", vp[:, 8:10, hh, 0:64], cvv[:, :, c0:c0 + 64], writes=[b_Vc[i][hh]])

        def prep_head(hd):
            j, hh = divmod(hd, 2)
            hb = hd % 2
            kp, vp = KV[j % 2]
            tb = tabb[hb].rearrange("p (e c) -> p e c", e=30)
            P.dma("pool", tb[0:64, 7:22, :], tab_d[l, hd].rearrange("p (e c) -> p e c", e=15), writes=[b_tab[hb]])
            P.dma("pool", tb[64:128, 8:23, :], tab_d[l, hd].rearrange("p (e c) -> p e c", e=15), writes=[b_tab2[hb]])
            if hh == 1 and j + 1 < 8:
                load_pair(j + 1)
            b_kTc = b_kTa_c[hb]
            bt, pt = next_psT()
            fns = [trf(pt[0:64, c * 128:(c + 1) * 128], kp[:, 8 + c, hh * 64:(hh + 1) * 64], ident_bf) for c in range(2)]
            P.pe_group(fns, reads=[b_const, b_Kc[j % 2]], writes=[bt])
            copy_op("dve", kTa[hb][0:64, 1024:1280], pt[0:64, 0:256], [bt], [b_kTc])
            bt, pt = next_psT()
            fns = [trf(pt[0:64, tt * 128:(tt + 1) * 128], qn[:, tt, hd * 64:(hd + 1) * 64], ident_bf) for tt in range(8)]
            P.pe_group(fns, reads=[b_const] + b_qn, writes=[bt])
            copy_op("dve", qTa[hb][0:64, :], pt[0:64, :], [bt], [b_qTa[hb]])
            bt, pt = next_psT()
            fns = [trf(pt[0:64, tt * 128:(tt + 1) * 128], kp[:, tt, hh * 64:(hh + 1) * 64], ident_bf) for tt in range(8)]
            P.pe_group(fns, reads=[b_const, b_K[j % 2]], writes=[bt])
            copy_op("act", kTa[hb][0:64, 0:1024], pt[0:64, :], [bt], [b_kTa[hb]])

        def attend(hd, th):
            j, hh = divmod(hd, 2)
            hb = hd % 2
            kp, vp = KV[j % 2]
            oi = occ[0] % 2
            occ[0] += 1
            bo, pso = b_psO[oi], psO[oi]
            kcs = KCS[th]

            def S(kc):
                bp, ps = next_psS()
                fns = [mmf(ps[:, :], kTa[hb][0:81, kc * 128:(kc + 1) * 128], qTa[hb][0:81, th * 512:(th + 1) * 512], True, kc >= 8)]
                rds = [b_kTa[hb], b_kTa_c[hb], b_qTa[hb], b_qx[hb], b_kx[hb]]
                if kc < 8:
                    s0 = 8 * th - 2 * kc + 14
                    fns.append(mmf(ps[:, :], ident8, tabb[hb][:, s0 * 64:(s0 + 8) * 64], False, True))
                    rds += [b_tab[hb], b_tab2[hb], b_const]
                P.pe_group(fns, reads=rds, writes=[bp])
                i = ptc[0] % NPT
                ptc[0] += 1
                pti = PT[i]
                P.op("act", lambda h, pti=pti, ps=ps: h.activation(out=pti, in_=ps[:, :], func=AF.Exp, bias=NEGB, scale=0.125),
                     reads=[bp], writes=[b_PT[i]])
                return i

            def PV(n, kc, i):
                fns = [mmf(pso[:, :], vp[:, kc, hh, :], PT[i], n == 0, n == len(kcs) - 1)]
                P.pe_group(fns, reads=[b_PT[i], b_V[j % 2][hh], b_Vc[j % 2][hh]], writes=[bo])

            idx = {}
            for n in range(3):
                idx[n] = S(kcs[n])
            for n in range(len(kcs)):
                PV(n, kcs[n], idx[n])
                if n + 3 < len(kcs):
                    idx[n + 3] = S(kcs[n + 3])
            P.op("dve", lambda h, pso=pso: h.reciprocal(out=rd[0:64, :], in_=pso[64:128, :]), reads=[bo], writes=[b_rd])
            P.op("dve", lambda h, pso=pso, hh=hh, j=j, th=th: h.tensor_tensor(
                out=OT[hh * 64:(hh + 1) * 64, j, th * 512:(th + 1) * 512], in0=pso[0:64, :], in1=rd[0:64, :], op=ALU.mult),
                reads=[bo, b_rd], writes=[b_OT[j]])

        load_pair(0)
        prep_head(0)
        for hd in range(16):
            attend(hd, 0)
            if hd + 1 < 16:
                prep_head(hd + 1)
            attend(hd, 1)
        if stop == 'p3' and l == 1:
            mod_state['in_attn'] = False
            return
        mod_state["in_attn"] = False
        mod_require(l, 2)
        P.barrier()
        hT = bv(O_HT, 8192).rearrange("p (c t) -> p c t", c=16)
        b_hT = [P.buf(f"hTb{c}") for c in range(16)]
        mod_idle(2)
        make_h(l, 0, hT, b_hT, O_SC)
        naT = bv(O_QN, 2048).rearrange("p (c t) -> p c t", c=4)
        fnT = bv(O_QN + 2048, 2048).rearrange("p (c t) -> p c t", c=4)
        b_na = [P.buf(f"na{c}") for c in range(4)]
        b_fn = [P.buf(f"fn{c}") for c in range(4)]
        sg = [bv(O_SC + 1536 + i * 256, 256) for i in range(3)]
        b_sg = [P.buf(f"sg{i}") for i in range(3)]
        sgc = [0]
        g1 = mods[l][:, 32:48]
        wnav = wna_d[l].rearrange("(kc p) n -> p kc n", p=128)
        wfnv = wfn_d[l].rearrange("(kc p) n -> p kc n", p=128)
        wgv = wgate_d[l].rearrange("(kc p) n -> p kc n", p=128)
        for c4 in range(4):
            for (wv, src, b_src, dst, b_dst) in ((wnav, OT, b_OT, naT, b_na), (wfnv, FT, b_FT, fnT, b_fn)):
                bs, sv = wload(wv[:, :, c4 * 512:(c4 + 1) * 512], "p (k n) -> p k n", nelem=4096, k=8)
                for jj in range(4):
                    for th in range(2):
                        bp, ps = next_psS()
                        fns = [mmf(ps[:, :], sv[:, kc, jj * 128:(jj + 1) * 128], src[:, kc, th * 512:(th + 1) * 512], kc == 0, kc == 7) for kc in range(8)]
                        P.pe_group(fns, reads=[bs] + b_src, writes=[bp])
                        copy_op(evac_engine(), dst[:, jj, th * 512:(th + 1) * 512], ps[:, :], [bp], [b_dst[jj]])
                heavy_done()
            for gi in range(2):
                col0 = gi * 2048 + c4 * 512
                bs, sv = wload(wgv[:, :, col0:col0 + 512], "p (k n) -> p k n", k=16)
                for jj in range(4):
                    for th in range(2):
                        bp, ps = next_psS()
                        fns = [mmf(ps[:, :], sv[:, kc, jj * 128:(jj + 1) * 128], hT[:, kc, th * 512:(th + 1) * 512], kc == 0, kc == 15) for kc in range(16)]
                        P.pe_group(fns, reads=[bs] + b_hT, writes=[bp])
                        i = sgc[0] % 3
                        sgc[0] += 1
                        sgi = sg[i]
                        P.op("act", lambda h, sgi=sgi, ps=ps: h.activation(out=sgi, in_=ps[:, :], func=AF.Sigmoid), reads=[bp], writes=[b_sg[i]])
                        nsl = naT[:, jj, th * 512:(th + 1) * 512]
                        if gi == 0:
                            P.op("dve", lambda h, sgi=sgi, nsl=nsl: h.tensor_tensor(out=nsl, in0=sgi, in1=nsl, op=ALU.mult),
                                 reads=[b_sg[i], b_na[jj]], writes=[b_na[jj]])
                        else:
                            fsl = fnT[:, jj, th * 512:(th + 1) * 512]
                            P.op("dve", lambda h, sgi=sgi, fsl=fsl: h.tensor_tensor(out=sgi, in0=sgi, in1=fsl, op=ALU.mult),
                                 reads=[b_fn[jj]], writes=[b_sg[i]])
                            P.op("dve", lambda h, sgi=sgi, nsl=nsl: h.tensor_tensor(out=nsl, in0=sgi, in1=nsl, op=ALU.add),
                                 reads=[b_sg[i], b_na[jj]], writes=[b_na[jj]])
                heavy_done()
            wov = wo_d[l][c4 * 512:(c4 + 1) * 512, :].rearrange("(kc p) n -> p kc n", p=128)
            bs, sv = wload(wov, "p (k n) -> p k n", k=4)
            for co in range(16):
                for th in range(2):
                    bp, ps = next_psS()
                    fns = [mmf(ps[:, :], sv[:, kc, co * 128:(co + 1) * 128], naT[:, kc, th * 512:(th + 1) * 512], kc == 0, kc == 3) for kc in range(4)]
                    P.pe_group(fns, reads=[bs] + b_na, writes=[bp])
                    xs = XT[:, co, th * 512:(th + 1) * 512]
                    P.op("dve", lambda h, xs=xs, ps=ps, co=co: h.scalar_tensor_tensor(
                        out=xs, in0=ps[:, :], scalar=g1[:, co:co + 1], in1=xs, op0=ALU.mult, op1=ALU.add),
                        reads=[bp, b_mq[l][2]], writes=[b_xT[co]])
            heavy_done()
        if stop == 'p6' and l == 1:
            return
        mod_require(l, 3)
        mod_require(l, 4)
        P.barrier()
        hT = bv(O_HT, 8192).rearrange("p (c t) -> p c t", c=16)
        b_hT = [P.buf(f"h2T{c}") for c in range(16)]
        norm_stats(l, O_SC)
        mod_idle(4)
        make_h(l, 1, hT, b_hT, O_SC + 768)
        mod_require(l, 5)
        actT = bv(O_QN, 4096).rearrange("p (c t) -> p c t", c=8)
        b_act = [P.buf(f"act{c}") for c in range(8)]
        a_sb = bv(O_OT, 2048).rearrange("p (c t) -> p c t", c=4)
        b_asb = [P.buf(f"asb{c}") for c in range(4)]
        sgf = [fv(O_SC + 2304 + i * 512, 512) for i in range(3)]
        b_sgf = [P.buf(f"sgf{i}") for i in range(3)]
        g2 = mods[l][:, 80:96]
        wguv = wgu_d[l].rearrange("(kc p) n -> p kc n", p=128)
        sgc = [0]
        nf = DFF // 512
        f = 0
        while f < nf:
            nfg = min(2, nf - f)
            for fi in range(nfg):
                ff = f + fi
                bs, sv = wload(wguv[:, :, ff * 512:(ff + 1) * 512], "p (k n) -> p k n", k=16)
                for j in range(4):
                    for th in range(2):
                        bp, ps = next_psS()
                        fns = [mmf(ps[:, :], sv[:, kc, j * 128:(j + 1) * 128], hT[:, kc, th * 512:(th + 1) * 512], kc == 0, kc == 15) for kc in range(16)]
                        P.pe_group(fns, reads=[bs] + b_hT, writes=[bp])
                        copy_op(evac_engine(), a_sb[:, j, th * 512:(th + 1) * 512], ps[:, :], [bp], [b_asb[j]])
                heavy_done()
                bs, sv = wload(wguv[:, :, DFF + ff * 512:DFF + (ff + 1) * 512], "p (k n) -> p k n", k=16)
                for j in range(4):
                    ci = fi * 4 + j
                    for th in range(2):
                        bp, ps = next_psS()
                        fns = [mmf(ps[:, :], sv[:, kc, j * 128:(j + 1) * 128], hT[:, kc, th * 512:(th + 1) * 512], kc == 0, kc == 15) for kc in range(16)]
                        P.pe_group(fns, reads=[bs] + b_hT, writes=[bp])
                        i = sgc[0] % 3
                        sgc[0] += 1
                        sgi = sgf[i]
                        P.op("act", lambda h, sgi=sgi, ps=ps: h.activation(out=sgi, in_=ps[:, :], func=AF.Silu), reads=[bp], writes=[b_sgf[i]])
                        P.op("dve", lambda h, sgi=sgi, j=j, ci=ci, th=th: h.tensor_tensor(
                            out=actT[:, ci, th * 512:(th + 1) * 512], in0=sgi, in1=a_sb[:, j, th * 512:(th + 1) * 512], op=ALU.mult),
                            reads=[b_sgf[i], b_asb[j]], writes=[b_act[ci]])
                heavy_done()
            ng_ = nfg * 4
            wdv = wdn_d[l][f * 512:f * 512 + ng_ * 128, :].rearrange("(kc p) n -> p kc n", p=128)
            ncols = 8192 // ng_
            for cbd in range(D // ncols):
                bs, sv = wload(wdv[:, :, cbd * ncols:(cbd + 1) * ncols], "p (k n) -> p k n", k=ng_)
                for cq in range(ncols // 128):
                    co = cbd * (ncols // 128) + cq
                    for th in range(2):
                        bp, ps = next_psS()
                        fns = [mmf(ps[:, :], sv[:, kc, cq * 128:(cq + 1) * 128], actT[:, kc, th * 512:(th + 1) * 512], kc == 0, kc == ng_ - 1) for kc in range(ng_)]
                        P.pe_group(fns, reads=[bs] + b_act[:ng_], writes=[bp])
                        xs = XT[:, co, th * 512:(th + 1) * 512]
                        P.op("dve", lambda h, xs=xs, ps=ps, co=co: h.scalar_tensor_tensor(
                            out=xs, in0=ps[:, :], scalar=g2[:, co:co + 1], in1=xs, op0=ALU.mult, op1=ALU.add),
                            reads=[bp, b_mq[l][5]], writes=[b_xT[co]])
                heavy_done()
            f += nfg

    for l in range(depth):
        layer(l)

    P.barrier()
    ost = [fv(R0 + i * 512, 512) for i in range(4)]
    b_ost = [P.buf(f"ost{i}") for i in range(4)]
    b_oy = Buf("oy")
    k = 0
    for tt in range(8):
        for dq in range(4):
            bp, ps = next_psS()
            fns = [trf(ps[:, j * 128:(j + 1) * 128], XT[:, dq * 4 + j, tt * 128:(tt + 1) * 128], identf) for j in range(4)]
            P.pe_group(fns, reads=[b_const] + b_xT[dq * 4:dq * 4 + 4], writes=[bp])
            i = k % 4
            k += 1
            copy_op(evac_engine(), ost[i], ps[:, :], [bp], [b_ost[i]])
            P.dma_add("sp", oy_d[tt * 128:(tt + 1) * 128, dq * 512:(dq + 1) * 512], ost[i], b_oy, reads=[b_ost[i]])
    sp = P.eng["sp"]
    for t in sp.dlast:
        if t is not None:
            P.need(sp, t)

    sems = {}
    for n in ("pe", "act", "dve", "pool", "sp"):
        sems[("e", n)] = es.enter_context(nc.semaphore(f"s_{n}"))
    for q in ("pool", "sp"):
        for k in range(NDS):
            sems[("d", q, k)] = es.enter_context(nc.semaphore(f"d_{q}{k}"))

    def replay(e, h):
        own = sems[("e", e.name)]
        for o_ in e.ops:
            if o_[0] == "wait":
                h.wait_ge(sems[o_[1]], o_[2])
            elif o_[0] == "op":
                ins = o_[1](h)
                if o_[2]:
                    ins.then_inc(own, 1)
            else:
                h.dma_start(out=o_[1], in_=o_[2]).then_inc(sems[o_[3]], 16)

    with nc.Block() as block:
        @block.tensor
        def _(h):
            replay(P.eng["pe"], h)

        @block.scalar
        def _(h):
            replay(P.eng["act"], h)

        @block.vector
        def _(h):
            replay(P.eng["dve"], h)

        @block.gpsimd
        def _(h):
            replay(P.eng["pool"], h)

        @block.sync
        def _(h):
            replay(P.eng["sp"], h)
    es.close()
    return nc


def _host_consts():
    identf = np.eye(128, dtype=np.float32)
    ident8 = (8.0 * np.eye(128)).astype(np.float32)
    ch = np.arange(256)
    ang = 2.0 * np.pi * np.outer(ch, ch) / 256.0
    cs = np.concatenate([np.cos(ang), np.sin(ang)], axis=1) / 16.0
    return identf, ident8, cs.astype(np.float32)


def _dft_tables(seq):
    t = np.arange(seq)
    ang = 2.0 * np.pi * np.outer(t, t) / seq
    c = np.cos(ang) / np.sqrt(seq)
    s = -np.sin(ang) / np.sqrt(seq)
    nb = T // seq
    ct = np.zeros((T, T), np.float32)
    nst = np.zeros((T, T), np.float32)
    for b in range(nb):
        ct[b * seq:(b + 1) * seq, b * seq:(b + 1) * seq] = c
        nst[b * seq:(b + 1) * seq, b * seq:(b + 1) * seq] = s
    return ct, nst


def _mask_feats(sample):
    qx = np.zeros((17, 1024), np.float32)
    kx = np.zeros((17, 1280), np.float32)
    rows = np.arange(1024) // 64
    for j in range(16):
        kx[j, :1024] = (rows == j)
    kx[16, 1024:] = 1.0
    if sample:
        rs = np.clip(rows - 4, 0, 8)
        for j in range(16):
            qx[j] = 512.0 * ((rs <= j) & (j <= rs + 7))
        qx[16] = 512.0
    else:
        for j in range(16):
            qx[j] = 512.0 * ((rows // 4) == (j // 4))
    return qx, kx


def _bias_table(rpb):
    cp = np.arange(64)[:, None]
    c = np.arange(64)[None, :]
    ws = np.clip(c - 8, 0, 48)
    inwin = (cp >= ws) & (cp < ws + 16)
    dc = np.clip(cp - c + 15, 0, 30)
    tab = np.empty((2, 16, 64, 15, 64), np.float32)
    for e in range(15):
        g = rpb[:, :, 14 - e, :][:, :, dc]
        tab[:, :, :, e, :] = np.where(inwin[None, None], g, np.float32(NEGB))
    return tab.reshape(2, 16, 64, 15 * 64)


_NC_CACHE = {}


def kernel(x_prompt, x_sample, cache_k, cache_v, c, c_ctx, w_mod, b_mod, norm1_g, norm2_g,
           w_in, q_norm_g, k_norm_g, rpb, w_na_proj, w_fnet_proj, w_gate, w_o, w_gate_up, w_down):
    f = lambda a: np.ascontiguousarray(np.asarray(a, dtype=np.float32))
    x_prompt, x_sample, cache_k, cache_v, c, c_ctx = map(f, (x_prompt, x_sample, cache_k, cache_v, c, c_ctx))
    w_mod, b_mod, norm1_g, norm2_g, w_in, q_norm_g, k_norm_g, rpb = map(f, (w_mod, b_mod, norm1_g, norm2_g, w_in, q_norm_g, k_norm_g, rpb))
    w_na_proj, w_fnet_proj, w_gate, w_o, w_gate_up, w_down = map(f, (w_na_proj, w_fnet_proj, w_gate, w_o, w_gate_up, w_down))

    identf, ident8, cs = _host_consts()
    ct_s, nst_s = _dft_tables(1024)
    ct_p, nst_p = _dft_tables(256)
    qx_s, kx_s = _mask_feats(True)
    qx_p, kx_p = _mask_feats(False)
    tab_s = _bias_table(rpb)
    tab_p = np.zeros_like(tab_s)
    bmod_l = f(b_mod.reshape(2, 96, 128).transpose(0, 2, 1))
    n1g_l = f(norm1_g.reshape(2, 16, 128).transpose(0, 2, 1))
    n2g_l = f(norm2_g.reshape(2, 16, 128).transpose(0, 2, 1))
    qg_l = f(np.broadcast_to(q_norm_g[:, None, :], (2, 128, 64)))
    kg_l = f(np.broadcast_to(k_norm_g[:, None, :], (2, 128, 64)))
    zkv = np.zeros((2, 256, 1024), np.float32)
    shared = {"bmod": bmod_l, "n1g": n1g_l, "n2g": n2g_l, "qg": qg_l, "kg": kg_l, "w_mod": w_mod, "w_in": w_in,
              "w_na": w_na_proj, "w_fn": w_fnet_proj, "w_gate": w_gate, "w_o": w_o, "w_gu": w_gate_up, "w_dn": w_down,
              "cs": cs, "identf": identf, "ident8": ident8}
    in_maps = []
    for core in range(8):
        m = dict(shared)
        if core < 4:
            m["x"] = f(x_prompt[4 * core:4 * core + 4].reshape(T, D))
            cond = c_ctx
            m["ck"] = zkv; m["cv"] = zkv
            m["tab"] = tab_p; m["qx"] = qx_p; m["kx"] = kx_p; m["ct"] = ct_p; m["nst"] = nst_p
        else:
            b = core - 4
            m["x"] = f(x_sample[b])
            cond = c[b]
            m["ck"] = f(cache_k[b].reshape(2, 256, 1024)); m["cv"] = f(cache_v[b].reshape(2, 256, 1024))
            m["tab"] = tab_s; m["qx"] = qx_s; m["kx"] = kx_s; m["ct"] = ct_s; m["nst"] = nst_s
        m["cond"] = f(cond.reshape(16, 128).T)
        in_maps.append(m)
    if "nc" not in _NC_CACHE:
        _NC_CACHE["nc"] = build_program(2)
    nc = _NC_CACHE["nc"]
    res = run_bass_kernel_spmd(nc, in_maps, core_ids=list(range(8)))
    rs = res.results
    y_prompt = np.concatenate([rs[i]["oy"].reshape(4, 256, D) for i in range(4)], axis=0).astype(np.float32)
    y_sample = np.stack([rs[4 + b]["oy"] for b in range(4)], axis=0).astype(np.float32)
    new_k = np.concatenate([rs[i]["ok"].reshape(2, 4, 256, 16, 64).transpose(1, 0, 2, 3, 4) for i in range(4)], axis=0).astype(np.float32)
    new_v = np.concatenate([rs[i]["ov"].reshape(2, 4, 256, 16, 64).transpose(1, 0, 2, 3, 4) for i in range(4)], axis=0).astype(np.float32)
    return (y_prompt, y_sample, np.ascontiguousarray(new_k), np.ascontiguousarray(new_v))
```

```python
import numpy as np
from contextlib import ExitStack
import concourse.bass as bass
import concourse.mybir as mybir
from concourse.bass_utils import run_bass_kernel_spmd

F32 = mybir.dt.float32
BF16 = mybir.dt.bfloat16
ALU = mybir.AluOpType
AF = mybir.ActivationFunctionType
AX = mybir.AxisListType

D = 2048
T = 1024
NDC = 16
DFF = 5632
NDS = 12
NSLOT = 2
SLOTW = 4096
EPS = 1e-6
NEGB = -64.0


class Buf:
    __slots__ = ("w", "r", "name")

    def __init__(self, name, init=()):
        self.name = name
        self.w = list(init)
        self.r = []


class Rec:
    def __init__(self, name):
        self.name = name
        self.ops = []
        self.count = 0
        self.seen = {}
        self.dk = 0
        self.dlast = [None] * NDS
        self.dval = [0] * NDS


class Prog:
    def __init__(self):
        self.eng = {n: Rec(n) for n in ("pe", "act", "dve", "pool", "sp")}
        self.epoch = []

    def need(self, e, tok):
        key, val = tok
        if key == ("e", "pe") and e.name == "pe":
            return
        if e.seen.get(key, 0) >= val:
            return
        e.seen[key] = val
        e.ops.append(("wait", key, val))

    def _deps(self, e, reads, writes):
        for b in reads:
            for t in b.w:
                self.need(e, t)
        for b in writes:
            for t in b.w:
                self.need(e, t)
            for t in b.r:
                self.need(e, t)

    @staticmethod
    def _addr(b, tok):
        for i, t in enumerate(b.r):
            if t[0] == tok[0]:
                if t[1] < tok[1]:
                    b.r[i] = tok
                return
        b.r.append(tok)

    def _upd(self, tok, reads, writes):
        for b in reads:
            self._addr(b, tok)
        for b in writes:
            b.w = [tok]
            b.r = []

    def op(self, en, fn, reads=(), writes=()):
        e = self.eng[en]
        self._deps(e, reads, writes)
        e.count += 1
        tok = (("e", en), e.count)
        e.ops.append(("op", fn, True))
        self._upd(tok, reads, writes)
        return tok

    def pe_group(self, fns, reads=(), writes=()):
        e = self.eng["pe"]
        self._deps(e, reads, writes)
        for fn in fns[:-1]:
            e.ops.append(("op", fn, False))
        e.count += 1
        tok = (("e", "pe"), e.count)
        e.ops.append(("op", fns[-1], True))
        self._upd(tok, reads, writes)
        return tok

    def dma(self, q, out, in_, reads=(), writes=()):
        e = self.eng[q]
        k = e.dk % NDS
        e.dk += 1
        key = ("d", q, k)
        if e.dlast[k] is not None:
            self.need(e, e.dlast[k])
        self._deps(e, reads, writes)
        e.dval[k] += 16
        tok = (key, e.dval[k])
        e.dlast[k] = tok
        e.ops.append(("dma", out, in_, key))
        self._upd(tok, reads, writes)
        return tok

    def dma_add(self, q, out, in_, group, reads=()):
        tok = self.dma(q, out, in_, reads=reads, writes=[Buf("_g")])
        group.w.append(tok)
        return tok

    def barrier(self):
        toks = []
        for n in ("pe", "act", "dve"):
            e = self.eng[n]
            if e.count:
                toks.append((("e", n), e.count))
        for q in ("pool", "sp"):
            e = self.eng[q]
            for t in e.dlast:
                if t is not None:
                    toks.append(t)
        self.epoch = toks

    def buf(self, name):
        return Buf(name, self.epoch)


def build_program(depth=2, stop=None):
    nc = bass.Bass("TRN2", target_bir_lowering=False)
    P = Prog()

    def din(name, shape):
        return nc.dram_tensor(name, list(shape), F32, kind="ExternalInput").ap()

    def dout(name, shape):
        return nc.dram_tensor(name, list(shape), F32, kind="ExternalOutput").ap()

    x_d = din("x", [T, D])
    cond_d = din("cond", [128, 16])
    bmod_d = din("bmod", [2, 128, 96])
    n1g_d = din("n1g", [2, 128, 16])
    n2g_d = din("n2g", [2, 128, 16])
    qg_d = din("qg", [2, 128, 64])
    kg_d = din("kg", [2, 128, 64])
    wmod_d = din("w_mod", [2, D, 6 * D])
    win_d = din("w_in", [2, D, 4096])
    wna_d = din("w_na", [2, 1024, D])
    wfn_d = din("w_fn", [2, 1024, D])
    wgate_d = din("w_gate", [2, D, 4096])
    wo_d = din("w_o", [2, D, D])
    wgu_d = din("w_gu", [2, D, 2 * DFF])
    wdn_d = din("w_dn", [2, DFF, D])
    ck_d = din("ck", [2, 256, 1024])
    cv_d = din("cv", [2, 256, 1024])
    tab_d = din("tab", [2, 16, 64, 15 * 64])
    qx_d = din("qx", [17, 1024])
    kx_d = din("kx", [17, 1280])
    ct_d = din("ct", [1024, 1024])
    nst_d = din("nst", [1024, 1024])
    cs_d = din("cs", [256, 512])
    idf_d = din("identf", [128, 128])
    id8_d = din("ident8", [128, 128])
    oy_d = dout("oy", [T, D])
    ok_d = dout("ok", [2, T, 1024])
    ov_d = dout("ov", [2, T, 1024])

    es = ExitStack()
    AW = 53080
    A = es.enter_context(nc.sbuf_tensor("arena", [128, AW], F32))
    psS = [es.enter_context(nc.psum_tensor(f"psS{i}", [128, 512], F32)) for i in range(4)]
    psO = [es.enter_context(nc.psum_tensor(f"psO{i}", [128, 512], F32)) for i in range(2)]
    psT = [es.enter_context(nc.psum_tensor(f"psT{i}", [128, 1024], BF16)) for i in range(2)]

    def fv(off, n):
        return A[:, off:off + n]

    def bv(off, nwords):
        return A[:, off:off + nwords].bitcast(BF16)

    o = 0
    XT = fv(o, 16384).rearrange("p (c t) -> p c t", c=16); o += 16384
    slots = []
    for i in range(NSLOT):
        slots.append(bv(o, SLOTW)); o += SLOTW
    mslots = []
    for i in range(2):
        mslots.append(bv(o, 1024)); o += 1024
    ident_bf = bv(o, 64); o += 64
    ident8 = bv(o, 64); o += 64
    ones_bf = bv(o, 64); o += 64
    identf = fv(o, 128); o += 128
    mods = []
    for l in range(2):
        mods.append(fv(o, 96)); o += 96
    bmods = []
    for l in range(2):
        bmods.append(fv(o, 96)); o += 96
    gm = [[None, None], [None, None]]
    for l in range(2):
        for i in range(2):
            gm[l][i] = fv(o, 16); o += 16
    ng = [[None, None], [None, None]]
    for l in range(2):
        for i in range(2):
            ng[l][i] = fv(o, 16); o += 16
    qgb = []
    kgb = []
    for l in range(2):
        qgb.append(fv(o, 64)); o += 64
        kgb.append(fv(o, 64)); o += 64
    cond_s = fv(o, 16); o += 16
    s_bf = bv(o, 8); o += 8
    rstd = fv(o, 1024); o += 1024
    R0 = o
    assert R0 <= 28760, R0
    R0 = 28760
    RW = AW - R0

    b_xT = [Buf(f"xT{c}") for c in range(16)]
    b_slot = [Buf(f"slot{i}") for i in range(NSLOT)]
    b_mslot = [Buf(f"mslot{i}") for i in range(2)]
    b_psS = [Buf(f"psS{i}") for i in range(4)]
    b_psO = [Buf(f"psO{i}") for i in range(2)]
    b_psT = [Buf(f"psT{i}") for i in range(2)]
    b_const = Buf("const")
    b_mod = [Buf("mod0"), Buf("mod1")]
    b_rstd = Buf("rstd")
    b_ok = [Buf("ok0"), Buf("ok1")]
    b_ov = [Buf("ov0"), Buf("ov1")]
    state = {"slot": 0, "psS": 0, "psT": 0, "ev": 0, "mslot": 0}

    def next_psS():
        i = state["psS"] % 4
        state["psS"] += 1
        return b_psS[i], psS[i]

    def next_psT():
        i = state["psT"] % 2
        state["psT"] += 1
        return b_psT[i], psT[i]

    def wload(dram_ap, pattern, nelem=2 * SLOTW, **kw):
        i = state["slot"] % NSLOT
        state["slot"] += 1
        view = slots[i][:, 0:nelem].rearrange(pattern, **kw)
        if isinstance(dram_ap, (list, tuple)):
            for si, dap in enumerate(dram_ap):
                P.dma("pool", view[:, :, si, :], dap, writes=[b_slot[i]])
        else:
            P.dma("pool", view, dram_ap, writes=[b_slot[i]])
        return b_slot[i], view

    def evac_engine():
        state["ev"] += 1
        return "act" if state["ev"] % 2 else "dve"

    def copy_op(en, out, in_, reads, writes):
        if en == "act":
            return P.op("act", lambda h: h.activation(out=out, in_=in_, func=AF.Copy), reads, writes)
        return P.op("dve", lambda h: h.tensor_copy(out=out, in_=in_), reads, writes)

    def mmf(out, lhsT, rhs, start, stop):
        return lambda h: h.matmul(out, lhsT, rhs, start=start, stop=stop)

    def trf(out, in_, ident):
        return lambda h: h.transpose(out, in_, ident)

    P.op("dve", lambda h: h.memset(ones_bf, 1.0), writes=[b_const])
    P.dma_add("sp", cond_s, cond_d, b_const)
    P.dma_add("sp", identf, idf_d, b_const)
    P.dma_add("pool", ident_bf, idf_d, b_const)
    P.dma_add("pool", ident8, id8_d, b_const)
    for l in range(2):
        P.dma_add("sp", bmods[l], bmod_d[l], b_const)
        P.dma_add("sp", ng[l][0], n1g_d[l], b_const)
        P.dma_add("sp", ng[l][1], n2g_d[l], b_const)
        P.dma_add("sp", qgb[l], qg_d[l], b_const)
        P.dma_add("sp", kgb[l], kg_d[l], b_const)
    P.op("act", lambda h: h.activation(out=s_bf, in_=cond_s, func=AF.Silu), reads=[b_const], writes=[b_const])

    P.barrier()
    xin = fv(R0, 16384).rearrange("p (t d) -> p t d", t=8)
    b_xin = [P.buf(f"xin{t}") for t in range(8)]
    xv = x_d.rearrange("(t p) d -> p t d", p=128)
    for t in range(8):
        P.dma("sp", xin[:, t, :], xv[:, t, :], writes=[b_xin[t]])

    mod_q = []
    for l_ in range(depth):
        for w_ in range(6):
            for b_ in range(16):
                mod_q.append((l_, w_, b_))
    mod_done = set()
    mod_state = {"in_attn": False, "hb": 0}
    b_mq = [[Buf(f"mod{l_}_{w_}") for w_ in range(6)] for l_ in range(2)]

    def mod_block():
        l_, w_, b_ = mod_q.pop(0)
        bp, ps = b_psO[0], psO[0]
        col0 = w_ * 2048 + b_ * 128
        i = state["mslot"] % 2
        state["mslot"] += 1
        sv = mslots[i].rearrange("p (k n) -> p k n", k=16)
        bs = b_mslot[i]
        P.dma("pool", sv, wmod_d[l_].rearrange("(kc p) n -> p kc n", p=128)[:, :, col0:col0 + 128], writes=[bs])
        fns = [mmf(ps[:, b_:b_ + 1], sv[:, kc, :], s_bf[:, kc:kc + 1], kc == 0, kc == 15) for kc in range(16)]
        P.pe_group(fns, reads=[bs, b_const], writes=[bp])
        if b_ == 15:
            mod_fin(l_, w_)

    def mod_fin(l_, w_):
        bp, ps = b_psO[0], psO[0]
        md = mods[l_]
        bm = bmods[l_]
        P.op("dve", lambda h: h.tensor_tensor(out=md[:, w_ * 16:(w_ + 1) * 16], in0=ps[:, 0:16], in1=bm[:, w_ * 16:(w_ + 1) * 16], op=ALU.add),
             reads=[bp, b_const], writes=[b_mq[l_][w_]])
        if w_ == 1:
            P.op("dve", lambda h: h.scalar_tensor_tensor(out=gm[l_][0], in0=md[:, 16:32], scalar=1.0, in1=ng[l_][0],
                                                          op0=ALU.add, op1=ALU.mult), reads=[b_const], writes=[b_mq[l_][1]])
        if w_ == 4:
            P.op("dve", lambda h: h.scalar_tensor_tensor(out=gm[l_][1], in0=md[:, 64:80], scalar=1.0, in1=ng[l_][1],
                                                          op0=ALU.add, op1=ALU.mult), reads=[b_const], writes=[b_mq[l_][4]])
        mod_done.add((l_, w_))

    def mod_unit_big(l_, w_):
        bp, ps = b_psO[0], psO[0]
        for b4 in range(4):
            col0 = w_ * 2048 + b4 * 512
            bs, sv = wload(wmod_d[l_].rearrange("(kc p) n -> p kc n", p=128)[:, :, col0:col0 + 512], "p (k n) -> p k n", k=16)
            fns = []
            for j in range(4):
                cc = b4 * 4 + j
                for kc in range(16):
                    fns.append(mmf(ps[:, cc:cc + 1], sv[:, kc, j * 128:(j + 1) * 128], s_bf[:, kc:kc + 1], kc == 0, kc == 15))
            P.pe_group(fns, reads=[bs, b_const], writes=[bp])
        mod_fin(l_, w_)
        mod_q[:] = [m for m in mod_q if not (m[0] == l_ and m[1] == w_)]

    def mod_require(l_, w_):
        while (l_, w_) not in mod_done:
            mod_block()

    def mod_finish_unit():
        while mod_q and mod_q[0][2] != 0:
            mod_block()

    def heavy_done():
        mod_state["hb"] += 1
        for _ in range(2):
            if mod_q and not mod_state["in_attn"]:
                mod_block()

    def mod_idle(n):
        return

    mod_unit_big(0, 0)
    mod_unit_big(0, 1)
    for th in range(2):
        for dc in range(16):
            bp, ps = next_psS()
            fns = [trf(ps[:, j * 128:(j + 1) * 128], xin[:, th * 4 + j, dc * 128:(dc + 1) * 128], identf) for j in range(4)]
            P.pe_group(fns, reads=[b_const] + b_xin[th * 4:th * 4 + 4], writes=[bp])
            copy_op(evac_engine(), XT[:, dc, th * 512:(th + 1) * 512], ps[:, :], [bp], [b_xT[dc]])

    O_HT = R0
    O_QN = R0 + 8192
    O_UT = R0 + 12288
    O_OT = R0 + 16384
    O_SC = R0 + 20480

    def norm_stats(l, sc_off):
        sq = [bv(sc_off + i * 256, 256) for i in range(3)]
        b_sq = [P.buf(f"sq{i}") for i in range(3)]
        k = 0
        for th in range(2):
            bp, ps = next_psS()
            for dc in range(16):
                i = k % 3
                k += 1
                sqi = sq[i]
                if dc % 8 in (0, 3, 6):
                    P.op("act", lambda h, sqi=sqi, dc=dc, th=th: h.activation(out=sqi, in_=XT[:, dc, th * 512:(th + 1) * 512], func=AF.Square),
                         reads=[b_xT[dc]], writes=[b_sq[i]])
                else:
                    P.op("dve", lambda h, sqi=sqi, dc=dc, th=th: h.tensor_tensor(out=sqi, in0=XT[:, dc, th * 512:(th + 1) * 512],
                                                                               in1=XT[:, dc, th * 512:(th + 1) * 512], op=ALU.mult),
                         reads=[b_xT[dc]], writes=[b_sq[i]])
                P.pe_group([mmf(ps[:, :], ones_bf, sqi, dc == 0, dc == 15)], reads=[b_sq[i], b_const], writes=[bp])
            rs = rstd[:, th * 512:(th + 1) * 512]
            P.op("act", lambda h, rs=rs, ps=ps: h.activation(out=rs, in_=ps[:, :], func=AF.Sqrt, bias=EPS, scale=1.0 / D),
                 reads=[bp], writes=[b_rstd])
            P.op("dve", lambda h, rs=rs: h.reciprocal(out=rs, in_=rs), reads=[b_rstd], writes=[b_rstd])

    def make_h(l, which, hT, b_hT, sc_off):
        g = gm[l][which]
        bmods_ = [b_mq[l][0], b_mq[l][1]] if which == 0 else [b_mq[l][3], b_mq[l][4]]
        sh = mods[l][:, (0 if which == 0 else 48):(16 if which == 0 else 64)]
        tmp = [fv(sc_off + i * 512, 512) for i in range(3)]
        b_tmp = [P.buf(f"tmp{i}") for i in range(3)]
        k = 0
        for dc in range(16):
            for th in range(2):
                i = k % 3
                k += 1
                tm = tmp[i]
                P.op("dve", lambda h, tm=tm, dc=dc, th=th: h.scalar_tensor_tensor(
                    out=tm, in0=XT[:, dc, th * 512:(th + 1) * 512], scalar=g[:, dc:dc + 1],
                    in1=rstd[:, th * 512:(th + 1) * 512], op0=ALU.mult, op1=ALU.mult),
                    reads=[b_xT[dc], b_rstd] + bmods_, writes=[b_tmp[i]])
                P.op("act", lambda h, tm=tm, dc=dc, th=th: h.activation(
                    out=hT[:, dc, th * 512:(th + 1) * 512], in_=tm, func=AF.Identity, bias=sh[:, dc:dc + 1], scale=1.0),
                    reads=[b_tmp[i]] + bmods_, writes=[b_hT[dc]])

    def layer(l):
        mod_require(l, 0)
        mod_require(l, 1)
        P.barrier()
        hT = bv(O_HT, 8192).rearrange("p (c t) -> p c t", c=16)
        b_hT = [P.buf(f"hT{c}") for c in range(16)]
        norm_stats(l, O_SC)
        mod_idle(4)
        make_h(l, 0, hT, b_hT, O_SC + 768)
        P.barrier()
        qn = bv(O_QN, 4096).rearrange("p (t f) -> p t f", t=8)
        b_qn = [P.buf(f"qn{t}") for t in range(8)]
        UT = bv(O_UT, 4096).rearrange("p (c t) -> p c t", c=8)
        b_UT = [P.buf(f"UT{c}") for c in range(8)]
        OT = bv(O_OT, 4096).rearrange("p (c t) -> p c t", c=8)
        b_OT = [P.buf(f"OT{c}") for c in range(8)]
        sc2 = O_SC
        sqf_ = [fv(sc2, 512), fv(sc2 + 512, 512)]
        tq_ = [fv(sc2 + 1024, 512), fv(sc2 + 1536, 512)]
        st_ = [fv(sc2 + 2048, 8), fv(sc2 + 2072, 8)]
        st2_ = [fv(sc2 + 2056, 8), fv(sc2 + 2080, 8)]
        st3_ = [fv(sc2 + 2064, 8), fv(sc2 + 2088, 8)]
        kst = [fv(sc2 + 2096 + i * 512, 512) for i in range(3)]
        b_sqf_ = [P.buf("sqf0"), P.buf("sqf1")]; b_tq_ = [P.buf("tq0"), P.buf("tq1")]; b_st_ = [P.buf("st0"), P.buf("st1")]
        nrm_i = [0]
        b_kst = [P.buf(f"kst{i}") for i in range(3)]
        kcount = [0]
        okv = ok_d[l].rearrange("(t p) f -> p t f", p=128)
        ovv = ov_d[l].rearrange("(t p) f -> p t f", p=128)
        winv = win_d[l].rearrange("(kc p) n -> p kc n", p=128)
        for part in range(3):
            for cb in range(2):
                col0 = part * 1024 + cb * 512
                bs, sv = wload(winv[:, :, col0:col0 + 512], "p (k n) -> p k n", k=16)
                for tt in range(8):
                    bp, ps = next_psS()
                    fns = [mmf(ps[:, :], hT[:, kc, tt * 128:(tt + 1) * 128], sv[:, kc, :], kc == 0, kc == 15) for kc in range(16)]
                    P.pe_group(fns, reads=[bs] + b_hT, writes=[bp])
                    pv = ps[:, :]
                    if part < 2:
                        ni = nrm_i[0] % 2
                        nrm_i[0] += 1
                        sqf = sqf_[ni]; tq = tq_[ni]; st = st_[ni]; st2 = st2_[ni]; st3 = st3_[ni]
                        b_sqf = b_sqf_[ni]; b_tq = b_tq_[ni]; b_st = b_st_[ni]
                        gbc = (qgb[l] if part == 0 else kgb[l]).unsqueeze(1).to_broadcast([128, 8, 64])
                        P.op("act", lambda h, pv=pv, sqf=sqf: h.activation(out=sqf, in_=pv, func=AF.Square), reads=[bp], writes=[b_sqf])
                        P.op("dve", lambda h, st=st, sqf=sqf: h.tensor_reduce(out=st, in_=sqf.rearrange("p (a b) -> p a b", a=8), axis=AX.X, op=ALU.add),
                             reads=[b_sqf], writes=[b_st])
                        P.op("act", lambda h, st=st, st2=st2: h.activation(out=st2, in_=st, func=AF.Sqrt, bias=EPS, scale=1.0 / 64), reads=[b_st], writes=[b_st])
                        P.op("dve", lambda h, st2=st2, st3=st3: h.reciprocal(out=st3, in_=st2), reads=[b_st], writes=[b_st])
                        P.op("dve", lambda h, pv=pv, tq=tq, st3=st3: h.tensor_tensor(out=tq.rearrange("p (a b) -> p a b", a=8),
                                                                     in0=pv.rearrange("p (a b) -> p a b", a=8),
                                                                     in1=st3.unsqueeze(2).to_broadcast([128, 8, 64]), op=ALU.mult),
                             reads=[bp, b_st], writes=[b_tq])
                        if part == 0:
                            P.op("dve", lambda h, tt=tt, cb=cb, gbc=gbc, tq=tq: h.tensor_tensor(
                                out=qn[:, tt, cb * 512:(cb + 1) * 512].rearrange("p (a b) -> p a b", a=8),
                                in0=tq.rearrange("p (a b) -> p a b", a=8), in1=gbc, op=ALU.mult),
                                reads=[b_tq, b_const], writes=[b_qn[tt]])
                        else:
                            i = kcount[0] % 3
                            kcount[0] += 1
                            ks = kst[i]
                            P.op("dve", lambda h, ks=ks, gbc=gbc, tq=tq: h.tensor_tensor(
                                out=ks.rearrange("p (a b) -> p a b", a=8),
                                in0=tq.rearrange("p (a b) -> p a b", a=8), in1=gbc, op=ALU.mult),
                                reads=[b_tq, b_const], writes=[b_kst[i]])
                            P.dma_add("sp", okv[:, tt, cb * 512:(cb + 1) * 512], ks, b_ok[l], reads=[b_kst[i]])
                    else:
                        i = kcount[0] % 3
                        kcount[0] += 1
                        ks = kst[i]
                        copy_op(evac_engine(), ks, pv, [bp], [b_kst[i]])
                        P.dma_add("sp", ovv[:, tt, cb * 512:(cb + 1) * 512], ks, b_ov[l], reads=[b_kst[i]])
                heavy_done()
        if stop == 'p2' and l == 1:
            return
        for cb in range(2):
            col0 = 3072 + cb * 512
            bs, sv = wload(winv[:, :, col0:col0 + 512], "p (k n) -> p k n", k=16)
            for j in range(4):
                uc = cb * 4 + j
                for th in range(2):
                    bp, ps = next_psS()
                    fns = [mmf(ps[:, :], sv[:, kc, j * 128:(j + 1) * 128], hT[:, kc, th * 512:(th + 1) * 512], kc == 0, kc == 15) for kc in range(16)]
                    P.pe_group(fns, reads=[bs] + b_hT, writes=[bp])
                    copy_op(evac_engine(), UT[:, uc, th * 512:(th + 1) * 512], ps[:, :], [bp], [b_UT[uc]])
            heavy_done()
        P.barrier()
        CT = bv(O_HT, 4096).rearrange("p (t n) -> p t n", t=8)
        NST = bv(O_HT + 4096, 4096).rearrange("p (t n) -> p t n", t=8)
        b_ct = P.buf("dft_ct"); b_nst = P.buf("dft_nst"); b_cs = P.buf("dft_cs")
        AB = bv(O_SC, 2048).rearrange("p (t n) -> p t n", t=8)
        CS = bv(O_SC + 2048, 512).rearrange("p (c n) -> p c n", c=2)
        b_AB = [P.buf(f"AB{t}") for t in range(8)]
        P.dma("pool", CS, cs_d.rearrange("(c p) n -> p c n", p=128), writes=[b_cs])
        P.dma("pool", CT, ct_d.rearrange("(t p) n -> p t n", p=128), writes=[b_ct])
        P.dma("pool", NST, nst_d.rearrange("(t p) n -> p t n", p=128), writes=[b_nst])
        mod_idle(2)
        for g in range(4):
            if g > 0:
                mod_idle(1)
            for tt in range(8):
                bp, ps = next_psS()
                fns = [mmf(ps[:, :], UT[:, 2 * g + c2, tt * 128:(tt + 1) * 128], CS[:, c2, :], c2 == 0, c2 == 1) for c2 in range(2)]
                P.pe_group(fns, reads=[b_cs, b_UT[2 * g], b_UT[2 * g + 1]], writes=[bp])
                copy_op(evac_engine(), AB[:, tt, :], ps[:, :], [bp], [b_AB[tt]])
            for j in range(2):
                for th in range(2):
                    bp, ps = next_psS()
                    fns = []
                    for tt in range(8):
                        fns.append(mmf(ps[:, :], AB[:, tt, j * 128:(j + 1) * 128], CT[:, tt, th * 512:(th + 1) * 512], tt == 0, False))
                        fns.append(mmf(ps[:, :], AB[:, tt, 256 + j * 128:256 + (j + 1) * 128], NST[:, tt, th * 512:(th + 1) * 512], False, tt == 7))
                    P.pe_group(fns, reads=[b_ct, b_nst] + b_AB, writes=[bp])
                    copy_op(evac_engine(), UT[:, 2 * g + j, th * 512:(th + 1) * 512], ps[:, :], [bp], [b_UT[2 * g + j]])
        FT = UT
        b_FT = b_UT
        if stop == 'p5' and l == 1:
            return
        mod_finish_unit()
        mod_state["in_attn"] = True
        P.barrier()
        o3 = O_HT
        KV = []
        vflat = []
        for i in range(2):
            kp = bv(o3, 640).rearrange("p (c f) -> p c f", c=10); o3 += 640
            vflat.append(bv(o3, 1280))
            vp = bv(o3, 1280).rearrange("p (c h f) -> p c h f", c=10, h=2); o3 += 1280
            KV.append((kp, vp))
        b_K = [P.buf("k0"), P.buf("k1")]
        b_Kc = [P.buf("kc0"), P.buf("kc1")]
        b_V = [[P.buf(f"v{i}{hh}") for hh in range(2)] for i in range(2)]
        b_Vc = [[P.buf(f"vc{i}{hh}") for hh in range(2)] for i in range(2)]
        qTa = []
        kTa = []
        for i in range(2):
            qTa.append(bv(o3, 512)); o3 += 512
            kTa.append(bv(o3, 640)); o3 += 640
        b_qTa = [P.buf("qTa0"), P.buf("qTa1")]
        b_kTa = [P.buf("kTa0"), P.buf("kTa1")]
        b_kTa_c = [P.buf("kTac0"), P.buf("kTac1")]
        b_qx = [P.buf("qx0"), P.buf("qx1")]
        b_kx = [P.buf("kx0"), P.buf("kx1")]
        NPT = 6
        PT = []
        for i in range(NPT):
            PT.append(bv(o3, 256)); o3 += 256
        b_PT = [P.buf(f"PT{i}") for i in range(NPT)]
        rd = fv(o3, 512); o3 += 512
        b_rd = P.buf("rd")
        assert o3 <= O_HT + 8192, o3 - O_HT
        tabb = [bv(O_SC, 960), bv(O_SC + 960, 960)]
        b_tab = [P.buf("tab0"), P.buf("tab1")]
        b_tab2 = [P.buf("tab0b"), P.buf("tab1b")]
        for i in range(2):
            P.op("dve", lambda h, i=i: h.memset(tabb[i], 0.0), writes=[b_tab[i], b_tab2[i]])
            P.op("dve", lambda h, i=i: h.memset(vflat[i], 1.0), writes=[b_V[i][0], b_V[i][1], b_Vc[i][0], b_Vc[i][1]])
            P.dma("pool", qTa[i][64:81, :], qx_d, writes=[b_qx[i]])
            P.dma("pool", kTa[i][64:81, :], kx_d, writes=[b_kx[i]])
        ckv = ck_d[l].rearrange("(c p) f -> p c f", p=128)
        cvv = cv_d[l].rearrange("(c p) f -> p c f", p=128)
        ptc = [0]
        occ = [0]
        KCS = {0: [0, 1, 2, 3, 4, 5, 8, 9], 1: [2, 3, 4, 5, 6, 7, 8, 9]}

        def load_pair(j):
            kp, vp = KV[j % 2]
            i = j % 2
            P.dma("pool", kp[:, 0:8, :], okv[:, :, j * 128:(j + 1) * 128], reads=[b_ok[l]], writes=[b_K[i]])
            P.dma("pool", kp[:, 8:10, :], ckv[:, :, j * 128:(j + 1) * 128], writes=[b_Kc[i]])
            for hh in range(2):
                c0 = j * 128 + hh * 64
                P.dma("pool", vp[:, 0:8, hh, 0:64], ovv[:, :, c0:c0 + 64], reads=[b_ov[l]], writes=[b_V[i][hh]])
                P.dma("pool", vp[:, 8:10, hh, 0:64], cvv[:, :, c0:c0 + 64], writes=[b_Vc[i][hh]])

        def prep_head(hd):
            j, hh = divmod(hd, 2)
            hb = hd % 2
            kp, vp = KV[j % 2]
            tb = tabb[hb].rearrange("p (e c) -> p e c", e=30)
            P.dma("pool", tb[0:64, 7:22, :], tab_d[l, hd].rearrange("p (e c) -> p e c", e=15), writes=[b_tab[hb]])
            P.dma("pool", tb[64:128, 8:23, :], tab_d[l, hd].rearrange("p (e c) -> p e c", e=15), writes=[b_tab2[hb]])
            if hh == 1 and j + 1 < 8:
                load_pair(j + 1)
            b_kTc = b_kTa_c[hb]
            bt, pt = next_psT()
            fns = [trf(pt[0:64, c * 128:(c + 1) * 128], kp[:, 8 + c, hh * 64:(hh + 1) * 64], ident_bf) for c in range(2)]
            P.pe_group(fns, reads=[b_const, b_Kc[j % 2]], writes=[bt])
            copy_op("act", kTa[hb][0:64, 1024:1280], pt[0:64, 0:256], [bt], [b_kTc])
            bt, pt = next_psT()
            fns = [trf(pt[0:64, tt * 128:(tt + 1) * 128], qn[:, tt, hd * 64:(hd + 1) * 64], ident_bf) for tt in range(8)]
            P.pe_group(fns, reads=[b_const] + b_qn, writes=[bt])
            copy_op("dve", qTa[hb][0:64, :], pt[0:64, :], [bt], [b_qTa[hb]])
            bt, pt = next_psT()
            fns = [trf(pt[0:64, tt * 128:(tt + 1) * 128], kp[:, tt, hh * 64:(hh + 1) * 64], ident_bf) for tt in range(8)]
            P.pe_group(fns, reads=[b_const, b_K[j % 2]], writes=[bt])
            copy_op("dve", kTa[hb][0:64, 0:1024], pt[0:64, :], [bt], [b_kTa[hb]])

        def attend(hd, th):
            j, hh = divmod(hd, 2)
            hb = hd % 2
            kp, vp = KV[j % 2]
            oi = occ[0] % 2
            occ[0] += 1
            bo, pso = b_psO[oi], psO[oi]
            kcs = KCS[th]

            def S(kc):
                bp, ps = next_psS()
                fns = [mmf(ps[:, :], kTa[hb][0:81, kc * 128:(kc + 1) * 128], qTa[hb][0:81, th * 512:(th + 1) * 512], True, kc >= 8)]
                rds = [b_kTa[hb], b_kTa_c[hb], b_qTa[hb], b_qx[hb], b_kx[hb]]
                if kc < 8:
                    s0 = 8 * th - 2 * kc + 14
                    fns.append(mmf(ps[:, :], ident8, tabb[hb][:, s0 * 64:(s0 + 8) * 64], False, True))
                    rds += [b_tab[hb], b_tab2[hb], b_const]
                P.pe_group(fns, reads=rds, writes=[bp])
                i = ptc[0] % NPT
                ptc[0] += 1
                pti = PT[i]
                P.op("act", lambda h, pti=pti, ps=ps: h.activation(out=pti, in_=ps[:, :], func=AF.Exp, bias=NEGB, scale=0.125),
                     reads=[bp], writes=[b_PT[i]])
                return i

            def PV(n, kc, i):
                fns = [mmf(pso[:, :], vp[:, kc, hh, :], PT[i], n == 0, n == len(kcs) - 1)]
                P.pe_group(fns, reads=[b_PT[i], b_V[j % 2][hh], b_Vc[j % 2][hh]], writes=[bo])

            idx = {}
            for n in range(3):
                idx[n] = S(kcs[n])
            for n in range(len(kcs)):
                PV(n, kcs[n], idx[n])
                if n + 3 < len(kcs):
                    idx[n + 3] = S(kcs[n + 3])
            P.op("dve", lambda h, pso=pso: h.reciprocal(out=rd[0:64, :], in_=pso[64:128, :]), reads=[bo], writes=[b_rd])
            P.op("dve", lambda h, pso=pso, hh=hh, j=j, th=th: h.tensor_tensor(
                out=OT[hh * 64:(hh + 1) * 64, j, th * 512:(th + 1) * 512], in0=pso[0:64, :], in1=rd[0:64, :], op=ALU.mult),
                reads=[bo, b_rd], writes=[b_OT[j]])

        load_pair(0)
        prep_head(0)
        for hd in range(16):
            attend(hd, 0)
            if hd + 1 < 16:
                prep_head(hd + 1)
            attend(hd, 1)
        if stop == 'p3' and l == 1:
            mod_state['in_attn'] = False
            return
        mod_state["in_attn"] = False
        mod_require(l, 2)
        P.barrier()
        hT = bv(O_HT, 8192).rearrange("p (c t) -> p c t", c=16)
        b_hT = [P.buf(f"hTb{c}") for c in range(16)]
        mod_idle(2)
        make_h(l, 0, hT, b_hT, O_SC)
        naT = bv(O_QN, 2048).rearrange("p (c t) -> p c t", c=4)
        fnT = bv(O_QN + 2048, 2048).rearrange("p (c t) -> p c t", c=4)
        b_na = [P.buf(f"na{c}") for c in range(4)]
        b_fn = [P.buf(f"fn{c}") for c in range(4)]
        sg = [bv(O_SC + 1536 + i * 256, 256) for i in range(3)]
        b_sg = [P.buf(f"sg{i}") for i in range(3)]
        sgc = [0]
        g1 = mods[l][:, 32:48]
        wnav = wna_d[l].rearrange("(kc p) n -> p kc n", p=128)
        wfnv = wfn_d[l].rearrange("(kc p) n -> p kc n", p=128)
        wgv = wgate_d[l].rearrange("(kc p) n -> p kc n", p=128)
        for c4 in range(4):
            for (wv, src, b_src, dst, b_dst) in ((wnav, OT, b_OT, naT, b_na), (wfnv, FT, b_FT, fnT, b_fn)):
                bs, sv = wload(wv[:, :, c4 * 512:(c4 + 1) * 512], "p (k n) -> p k n", nelem=4096, k=8)
                for jj in range(4):
                    for th in range(2):
                        bp, ps = next_psS()
                        fns = [mmf(ps[:, :], sv[:, kc, jj * 128:(jj + 1) * 128], src[:, kc, th * 512:(th + 1) * 512], kc == 0, kc == 7) for kc in range(8)]
                        P.pe_group(fns, reads=[bs] + b_src, writes=[bp])
                        copy_op(evac_engine(), dst[:, jj, th * 512:(th + 1) * 512], ps[:, :], [bp], [b_dst[jj]])
                heavy_done()
            for gi in range(2):
                col0 = gi * 2048 + c4 * 512
                bs, sv = wload(wgv[:, :, col0:col0 + 512], "p (k n) -> p k n", k=16)
                for jj in range(4):
                    for th in range(2):
                        bp, ps = next_psS()
                        fns = [mmf(ps[:, :], sv[:, kc, jj * 128:(jj + 1) * 128], hT[:, kc, th * 512:(th + 1) * 512], kc == 0, kc == 15) for kc in range(16)]
                        P.pe_group(fns, reads=[bs] + b_hT, writes=[bp])
                        i = sgc[0] % 3
                        sgc[0] += 1
                        sgi = sg[i]
                        P.op("act", lambda h, sgi=sgi, ps=ps: h.activation(out=sgi, in_=ps[:, :], func=AF.Sigmoid), reads=[bp], writes=[b_sg[i]])
                        nsl = naT[:, jj, th * 512:(th + 1) * 512]
                        if gi == 0:
                            P.op("dve", lambda h, sgi=sgi, nsl=nsl: h.tensor_tensor(out=nsl, in0=sgi, in1=nsl, op=ALU.mult),
                                 reads=[b_sg[i], b_na[jj]], writes=[b_na[jj]])
                        else:
                            fsl = fnT[:, jj, th * 512:(th + 1) * 512]
                            P.op("dve", lambda h, sgi=sgi, fsl=fsl: h.tensor_tensor(out=sgi, in0=sgi, in1=fsl, op=ALU.mult),
                                 reads=[b_fn[jj]], writes=[b_sg[i]])
                            P.op("dve", lambda h, sgi=sgi, nsl=nsl: h.tensor_tensor(out=nsl, in0=sgi, in1=nsl, op=ALU.add),
                                 reads=[b_sg[i], b_na[jj]], writes=[b_na[jj]])
                heavy_done()
            wov = wo_d[l][c4 * 512:(c4 + 1) * 512, :].rearrange("(kc p) n -> p kc n", p=128)
            bs, sv = wload(wov, "p (k n) -> p k n", k=4)
            for co in range(16):
                for th in range(2):
                    bp, ps = next_psS()
                    fns = [mmf(ps[:, :], sv[:, kc, co * 128:(co + 1) * 128], naT[:, kc, th * 512:(th + 1) * 512], kc == 0, kc == 3) for kc in range(4)]
                    P.pe_group(fns, reads=[bs] + b_na, writes=[bp])
                    xs = XT[:, co, th * 512:(th + 1) * 512]
                    P.op("dve", lambda h, xs=xs, ps=ps, co=co: h.scalar_tensor_tensor(
                        out=xs, in0=ps[:, :], scalar=g1[:, co:co + 1], in1=xs, op0=ALU.mult, op1=ALU.add),
                        reads=[bp, b_mq[l][2]], writes=[b_xT[co]])
            heavy_done()
        if stop == 'p6' and l == 1:
            return
        mod_require(l, 3)
        mod_require(l, 4)
        P.barrier()
        hT = bv(O_HT, 8192).rearrange("p (c t) -> p c t", c=16)
        b_hT = [P.buf(f"h2T{c}") for c in range(16)]
        norm_stats(l, O_SC)
        mod_idle(4)
        make_h(l, 1, hT, b_hT, O_SC + 768)
        mod_require(l, 5)
        actT = bv(O_QN, 4096).rearrange("p (c t) -> p c t", c=8)
        b_act = [P.buf(f"act{c}") for c in range(8)]
        a_sb = bv(O_OT, 2048).rearrange("p (c t) -> p c t", c=4)
        b_asb = [P.buf(f"asb{c}") for c in range(4)]
        sgf = [fv(O_SC + 2304 + i * 512, 512) for i in range(3)]
        b_sgf = [P.buf(f"sgf{i}") for i in range(3)]
        g2 = mods[l][:, 80:96]
        wguv = wgu_d[l].rearrange("(kc p) n -> p kc n", p=128)
        sgc = [0]
        nf = DFF // 512
        f = 0
        while f < nf:
            nfg = min(2, nf - f)
            for fi in range(nfg):
                ff = f + fi
                bs, sv = wload(wguv[:, :, ff * 512:(ff + 1) * 512], "p (k n) -> p k n", k=16)
                for j in range(4):
                    for th in range(2):
                        bp, ps = next_psS()
                        fns = [mmf(ps[:, :], sv[:, kc, j * 128:(j + 1) * 128], hT[:, kc, th * 512:(th + 1) * 512], kc == 0, kc == 15) for kc in range(16)]
                        P.pe_group(fns, reads=[bs] + b_hT, writes=[bp])
                        copy_op(evac_engine(), a_sb[:, j, th * 512:(th + 1) * 512], ps[:, :], [bp], [b_asb[j]])
                heavy_done()
                bs, sv = wload(wguv[:, :, DFF + ff * 512:DFF + (ff + 1) * 512], "p (k n) -> p k n", k=16)
                for j in range(4):
                    ci = fi * 4 + j
                    for th in range(2):
                        bp, ps = next_psS()
                        fns = [mmf(ps[:, :], sv[:, kc, j * 128:(j + 1) * 128], hT[:, kc, th * 512:(th + 1) * 512], kc == 0, kc == 15) for kc in range(16)]
                        P.pe_group(fns, reads=[bs] + b_hT, writes=[bp])
                        i = sgc[0] % 3
                        sgc[0] += 1
                        sgi = sgf[i]
                        P.op("act", lambda h, sgi=sgi, ps=ps: h.activation(out=sgi, in_=ps[:, :], func=AF.Silu), reads=[bp], writes=[b_sgf[i]])
                        P.op("dve", lambda h, sgi=sgi, j=j, ci=ci, th=th: h.tensor_tensor(
                            out=actT[:, ci, th * 512:(th + 1) * 512], in0=sgi, in1=a_sb[:, j, th * 512:(th + 1) * 512], op=ALU.mult),
                            reads=[b_sgf[i], b_asb[j]], writes=[b_act[ci]])
                heavy_done()
            ng_ = nfg * 4
            wdv = wdn_d[l][f * 512:f * 512 + ng_ * 128, :].rearrange("(kc p) n -> p kc n", p=128)
            ncols = 8192 // ng_
            for cbd in range(D // ncols):
                bs, sv = wload(wdv[:, :, cbd * ncols:(cbd + 1) * ncols], "p (k n) -> p k n", k=ng_)
                for cq in range(ncols // 128):
                    co = cbd * (ncols // 128) + cq
                    for th in range(2):
                        bp, ps = next_psS()
                        fns = [mmf(ps[:, :], sv[:, kc, cq * 128:(cq + 1) * 128], actT[:, kc, th * 512:(th + 1) * 512], kc == 0, kc == ng_ - 1) for kc in range(ng_)]
                        P.pe_group(fns, reads=[bs] + b_act[:ng_], writes=[bp])
                        xs = XT[:, co, th * 512:(th + 1) * 512]
                        P.op("dve", lambda h, xs=xs, ps=ps, co=co: h.scalar_tensor_tensor(
                            out=xs, in0=ps[:, :], scalar=g2[:, co:co + 1], in1=xs, op0=ALU.mult, op1=ALU.add),
                            reads=[bp, b_mq[l][5]], writes=[b_xT[co]])
                heavy_done()
            f += nfg

    for l in range(depth):
        layer(l)

    P.barrier()
    ost = [fv(R0 + i * 512, 512) for i in range(4)]
    b_ost = [P.buf(f"ost{i}") for i in range(4)]
    b_oy = Buf("oy")
    k = 0
    for tt in range(8):
        for dq in range(4):
            bp, ps = next_psS()
            fns = [trf(ps[:, j * 128:(j + 1) * 128], XT[:, dq * 4 + j, tt * 128:(tt + 1) * 128], identf) for j in range(4)]
            P.pe_group(fns, reads=[b_const] + b_xT[dq * 4:dq * 4 + 4], writes=[bp])
            i = k % 4
            k += 1
            copy_op(evac_engine(), ost[i], ps[:, :], [bp], [b_ost[i]])
            P.dma_add("sp", oy_d[tt * 128:(tt + 1) * 128, dq * 512:(dq + 1) * 512], ost[i], b_oy, reads=[b_ost[i]])
    sp = P.eng["sp"]
    for t in sp.dlast:
        if t is not None:
            P.need(sp, t)

    sems = {}
    for n in ("pe", "act", "dve", "pool", "sp"):
        sems[("e", n)] = es.enter_context(nc.semaphore(f"s_{n}"))
    for q in ("pool", "sp"):
        for k in range(NDS):
            sems[("d", q, k)] = es.enter_context(nc.semaphore(f"d_{q}{k}"))

    def replay(e, h):
        own = sems[("e", e.name)]
        for o_ in e.ops:
            if o_[0] == "wait":
                h.wait_ge(sems[o_[1]], o_[2])
            elif o_[0] == "op":
                ins = o_[1](h)
                if o_[2]:
                    ins.then_inc(own, 1)
            else:
                h.dma_start(out=o_[1], in_=o_[2]).then_inc(sems[o_[3]], 16)

    with nc.Block() as block:
        @block.tensor
        def _(h):
            replay(P.eng["pe"], h)

        @block.scalar
        def _(h):
            replay(P.eng["act"], h)

        @block.vector
        def _(h):
            replay(P.eng["dve"], h)

        @block.gpsimd
        def _(h):
            replay(P.eng["pool"], h)

        @block.sync
        def _(h):
            replay(P.eng["sp"], h)
    es.close()
    return nc


def _host_consts():
    identf = np.eye(128, dtype=np.float32)
    ident8 = (8.0 * np.eye(128)).astype(np.float32)
    ch = np.arange(256)
    ang = 2.0 * np.pi * np.outer(ch, ch) / 256.0
    cs = np.concatenate([np.cos(ang), np.sin(ang)], axis=1) / 16.0
    return identf, ident8, cs.astype(np.float32)


def _dft_tables(seq):
    t = np.arange(seq)
    ang = 2.0 * np.pi * np.outer(t, t) / seq
    c = np.cos(ang) / np.sqrt(seq)
    s = -np.sin(ang) / np.sqrt(seq)
    nb = T // seq
    ct = np.zeros((T, T), np.float32)
    nst = np.zeros((T, T), np.float32)
    for b in range(nb):
        ct[b * seq:(b + 1) * seq, b * seq:(b + 1) * seq] = c
        nst[b * seq:(b + 1) * seq, b * seq:(b + 1) * seq] = s
    return ct, nst


def _mask_feats(sample):
    qx = np.zeros((17, 1024), np.float32)
    kx = np.zeros((17, 1280), np.float32)
    rows = np.arange(1024) // 64
    for j in range(16):
        kx[j, :1024] = (rows == j)
    kx[16, 1024:] = 1.0
    if sample:
        rs = np.clip(rows - 4, 0, 8)
        for j in range(16):
            qx[j] = 512.0 * ((rs <= j) & (j <= rs + 7))
        qx[16] = 512.0
    else:
        for j in range(16):
            qx[j] = 512.0 * ((rows // 4) == (j // 4))
    return qx, kx


def _bias_table(rpb):
    cp = np.arange(64)[:, None]
    c = np.arange(64)[None, :]
    ws = np.clip(c - 8, 0, 48)
    inwin = (cp >= ws) & (cp < ws + 16)
    dc = np.clip(cp - c + 15, 0, 30)
    tab = np.empty((2, 16, 64, 15, 64), np.float32)
    for e in range(15):
        g = rpb[:, :, 14 - e, :][:, :, dc]
        tab[:, :, :, e, :] = np.where(inwin[None, None], g, np.float32(NEGB))
    return tab.reshape(2, 16, 64, 15 * 64)


_NC_CACHE = {}


def kernel(x_prompt, x_sample, cache_k, cache_v, c, c_ctx, w_mod, b_mod, norm1_g, norm2_g,
           w_in, q_norm_g, k_norm_g, rpb, w_na_proj, w_fnet_proj, w_gate, w_o, w_gate_up, w_down):
    f = lambda a: np.ascontiguousarray(np.asarray(a, dtype=np.float32))
    x_prompt, x_sample, cache_k, cache_v, c, c_ctx = map(f, (x_prompt, x_sample, cache_k, cache_v, c, c_ctx))
    w_mod, b_mod, norm1_g, norm2_g, w_in, q_norm_g, k_norm_g, rpb = map(f, (w_mod, b_mod, norm1_g, norm2_g, w_in, q_norm_g, k_norm_g, rpb))
    w_na_proj, w_fnet_proj, w_gate, w_o, w_gate_up, w_down = map(f, (w_na_proj, w_fnet_proj, w_gate, w_o, w_gate_up, w_down))

    identf, ident8, cs = _host_consts()
    ct_s, nst_s = _dft_tables(1024)
    ct_p, nst_p = _dft_tables(256)
    qx_s, kx_s = _mask_feats(True)
    qx_p, kx_p = _mask_feats(False)
    tab_s = _bias_table(rpb)
    tab_p = np.zeros_like(tab_s)
    bmod_l = f(b_mod.reshape(2, 96, 128).transpose(0, 2, 1))
    n1g_l = f(norm1_g.reshape(2, 16, 128).transpose(0, 2, 1))
    n2g_l = f(norm2_g.reshape(2, 16, 128).transpose(0, 2, 1))
    qg_l = f(np.broadcast_to(q_norm_g[:, None, :], (2, 128, 64)))
    kg_l = f(np.broadcast_to(k_norm_g[:, None, :], (2, 128, 64)))
    zkv = np.zeros((2, 256, 1024), np.float32)
    shared = {"bmod": bmod_l, "n1g": n1g_l, "n2g": n2g_l, "qg": qg_l, "kg": kg_l, "w_mod": w_mod, "w_in": w_in,
              "w_na": w_na_proj, "w_fn": w_fnet_proj, "w_gate": w_gate, "w_o": w_o, "w_gu": w_gate_up, "w_dn": w_down,
              "cs": cs, "identf": identf, "ident8": ident8}
    in_maps = []
    for core in range(8):
        m = dict(shared)
        if core < 4:
            m["x"] = f(x_prompt[4 * core:4 * core + 4].reshape(T, D))
            cond = c_ctx
            m["ck"] = zkv; m["cv"] = zkv
            m["tab"] = tab_p; m["qx"] = qx_p; m["kx"] = kx_p; m["ct"] = ct_p; m["nst"] = nst_p
        else:
            b = core - 4
            m["x"] = f(x_sample[b])
            cond = c[b]
            m["ck"] = f(cache_k[b].reshape(2, 256, 1024)); m["cv"] = f(cache_v[b].reshape(2, 256, 1024))
            m["tab"] = tab_s; m["qx"] = qx_s; m["kx"] = kx_s; m["ct"] = ct_s; m["nst"] = nst_s
        m["cond"] = f(cond.reshape(16, 128).T)
        in_maps.append(m)
    if "nc" not in _NC_CACHE:
        _NC_CACHE["nc"] = build_program(2)
    nc = _NC_CACHE["nc"]
    res = run_bass_kernel_spmd(nc, in_maps, core_ids=list(range(8)))
    rs = res.results
    y_prompt = np.concatenate([rs[i]["oy"].reshape(4, 256, D) for i in range(4)], axis=0).astype(np.float32)
    y_sample = np.stack([rs[4 + b]["oy"] for b in range(4)], axis=0).astype(np.float32)
    new_k = np.concatenate([rs[i]["ok"].reshape(2, 4, 256, 16, 64).transpose(1, 0, 2, 3, 4) for i in range(4)], axis=0).astype(np.float32)
    new_v = np.concatenate([rs[i]["ov"].reshape(2, 4, 256, 16, 64).transpose(1, 0, 2, 3, 4) for i in range(4)], axis=0).astype(np.float32)
    return (y_prompt, y_sample, np.ascontiguousarray(new_k), np.ascontiguousarray(new_v))
```

```python
import numpy as np
from contextlib import ExitStack
import concourse.bass as bass
import concourse.mybir as mybir
from concourse.bass_utils import run_bass_kernel_spmd

F32 = mybir.dt.float32
BF16 = mybir.dt.bfloat16
ALU = mybir.AluOpType
AF = mybir.ActivationFunctionType
AX = mybir.AxisListType

D = 2048
T = 1024
NDC = 16
DFF = 5632
NDS = 12
NSLOT = 2
SLOTW = 4096
EPS = 1e-6
NEGB = -64.0


class Buf:
    __slots__ = ("w", "r", "name")

    def __init__(self, name, init=()):
        self.name = name
        self.w = list(init)
        self.r = []


class Rec:
    def __init__(self, name):
        self.name = name
        self.ops = []
        self.count = 0
        self.seen = {}
        self.dk = 0
        self.dlast = [None] * NDS
        self.dval = [0] * NDS


class Prog:
    def __init__(self):
        self.eng = {n: Rec(n) for n in ("pe", "act", "dve", "pool", "sp")}
        self.epoch = []

    def need(self, e, tok):
        key, val = tok
        if key == ("e", "pe") and e.name == "pe":
            return
        if e.seen.get(key, 0) >= val:
            return
        e.seen[key] = val
        e.ops.append(("wait", key, val))

    def _deps(self, e, reads, writes):
        for b in reads:
            for t in b.w:
                self.need(e, t)
        for b in writes:
            for t in b.w:
                self.need(e, t)
            for t in b.r:
                self.need(e, t)

    @staticmethod
    def _addr(b, tok):
        for i, t in enumerate(b.r):
            if t[0] == tok[0]:
                if t[1] < tok[1]:
                    b.r[i] = tok
                return
        b.r.append(tok)

    def _upd(self, tok, reads, writes):
        for b in reads:
            self._addr(b, tok)
        for b in writes:
            b.w = [tok]
            b.r = []

    def op(self, en, fn, reads=(), writes=()):
        e = self.eng[en]
        self._deps(e, reads, writes)
        e.count += 1
        tok = (("e", en), e.count)
        e.ops.append(("op", fn, True))
        self._upd(tok, reads, writes)
        return tok

    def pe_group(self, fns, reads=(), writes=()):
        e = self.eng["pe"]
        self._deps(e, reads, writes)
        for fn in fns[:-1]:
            e.ops.append(("op", fn, False))
        e.count += 1
        tok = (("e", "pe"), e.count)
        e.ops.append(("op", fns[-1], True))
        self._upd(tok, reads, writes)
        return tok

    def dma(self, q, out, in_, reads=(), writes=()):
        e = self.eng[q]
        k = e.dk % NDS
        e.dk += 1
        key = ("d", q, k)
        if e.dlast[k] is not None:
            self.need(e, e.dlast[k])
        self._deps(e, reads, writes)
        e.dval[k] += 16
        tok = (key, e.dval[k])
        e.dlast[k] = tok
        e.ops.append(("dma", out, in_, key))
        self._upd(tok, reads, writes)
        return tok

    def dma_add(self, q, out, in_, group, reads=()):
        tok = self.dma(q, out, in_, reads=reads, writes=[Buf("_g")])
        group.w.append(tok)
        return tok

    def barrier(self):
        toks = []
        for n in ("pe", "act", "dve"):
            e = self.eng[n]
            if e.count:
                toks.append((("e", n), e.count))
        for q in ("pool", "sp"):
            e = self.eng[q]
            for t in e.dlast:
                if t is not None:
                    toks.append(t)
        self.epoch = toks

    def buf(self, name):
        return Buf(name, self.epoch)


def build_program(depth=2, stop=None):
    nc = bass.Bass("TRN2", target_bir_lowering=False)
    P = Prog()

    def din(name, shape):
        return nc.dram_tensor(name, list(shape), F32, kind="ExternalInput").ap()

    def dout(name, shape):
        return nc.dram_tensor(name, list(shape), F32, kind="ExternalOutput").ap()

    x_d = din("x", [T, D])
    cond_d = din("cond", [128, 16])
    bmod_d = din("bmod", [2, 128, 96])
    n1g_d = din("n1g", [2, 128, 16])
    n2g_d = din("n2g", [2, 128, 16])
    qg_d = din("qg", [2, 128, 64])
    kg_d = din("kg", [2, 128, 64])
    wmod_d = din("w_mod", [2, D, 6 * D])
    win_d = din("w_in", [2, D, 4096])
    wna_d = din("w_na", [2, 1024, D])
    wfn_d = din("w_fn", [2, 1024, D])
    wgate_d = din("w_gate", [2, D, 4096])
    wo_d = din("w_o", [2, D, D])
    wgu_d = din("w_gu", [2, D, 2 * DFF])
    wdn_d = din("w_dn", [2, DFF, D])
    ck_d = din("ck", [2, 256, 1024])
    cv_d = din("cv", [2, 256, 1024])
    tab_d = din("tab", [2, 16, 64, 15 * 64])
    qx_d = din("qx", [17, 1024])
    kx_d = din("kx", [17, 1280])
    ct_d = din("ct", [1024, 1024])
    nst_d = din("nst", [1024, 1024])
    cs_d = din("cs", [256, 512])
    idf_d = din("identf", [128, 128])
    id8_d = din("ident8", [128, 128])
    oy_d = dout("oy", [T, D])
    ok_d = dout("ok", [2, T, 1024])
    ov_d = dout("ov", [2, T, 1024])

    es = ExitStack()
    AW = 53080
    A = es.enter_context(nc.sbuf_tensor("arena", [128, AW], F32))
    psS = [es.enter_context(nc.psum_tensor(f"psS{i}", [128, 512], F32)) for i in range(4)]
    psO = [es.enter_context(nc.psum_tensor(f"psO{i}", [128, 512], F32)) for i in range(2)]
    psT = [es.enter_context(nc.psum_tensor(f"psT{i}", [128, 1024], BF16)) for i in range(2)]

    def fv(off, n):
        return A[:, off:off + n]

    def bv(off, nwords):
        return A[:, off:off + nwords].bitcast(BF16)

    o = 0
    XT = fv(o, 16384).rearrange("p (c t) -> p c t", c=16); o += 16384
    slots = []
    for i in range(NSLOT):
        slots.append(bv(o, SLOTW)); o += SLOTW
    mslots = []
    for i in range(2):
        mslots.append(bv(o, 1024)); o += 1024
    ident_bf = bv(o, 64); o += 64
    ident8 = bv(o, 64); o += 64
    ones_bf = bv(o, 64); o += 64
    identf = fv(o, 128); o += 128
    mods = []
    for l in range(2):
        mods.append(fv(o, 96)); o += 96
    bmods = []
    for l in range(2):
        bmods.append(fv(o, 96)); o += 96
    gm = [[None, None], [None, None]]
    for l in range(2):
        for i in range(2):
            gm[l][i] = fv(o, 16); o += 16
    ng = [[None, None], [None, None]]
    for l in range(2):
        for i in range(2):
            ng[l][i] = fv(o, 16); o += 16
    qgb = []
    kgb = []
    for l in range(2):
        qgb.append(fv(o, 64)); o += 64
        kgb.append(fv(o, 64)); o += 64
    cond_s = fv(o, 16); o += 16
    s_bf = bv(o, 8); o += 8
    rstd = fv(o, 1024); o += 1024
    R0 = o
    assert R0 <= 28760, R0
    R0 = 28760
    RW = AW - R0

    b_xT = [Buf(f"xT{c}") for c in range(16)]
    b_slot = [Buf(f"slot{i}") for i in range(NSLOT)]
    b_mslot = [Buf(f"mslot{i}") for i in range(2)]
    b_psS = [Buf(f"psS{i}") for i in range(4)]
    b_psO = [Buf(f"psO{i}") for i in range(2)]
    b_psT = [Buf(f"psT{i}") for i in range(2)]
    b_const = Buf("const")
    b_mod = [Buf("mod0"), Buf("mod1")]
    b_rstd = Buf("rstd")
    b_ok = [Buf("ok0"), Buf("ok1")]
    b_ov = [Buf("ov0"), Buf("ov1")]
    state = {"slot": 0, "psS": 0, "psT": 0, "ev": 0, "mslot": 0}

    def next_psS():
        i = state["psS"] % 4
        state["psS"] += 1
        return b_psS[i], psS[i]

    def next_psT():
        i = state["psT"] % 2
        state["psT"] += 1
        return b_psT[i], psT[i]

    def wload(dram_ap, pattern, nelem=2 * SLOTW, **kw):
        i = state["slot"] % NSLOT
        state["slot"] += 1
        view = slots[i][:, 0:nelem].rearrange(pattern, **kw)
        if isinstance(dram_ap, (list, tuple)):
            for si, dap in enumerate(dram_ap):
                P.dma("pool", view[:, :, si, :], dap, writes=[b_slot[i]])
        else:
            P.dma("pool", view, dram_ap, writes=[b_slot[i]])
        return b_slot[i], view

    def evac_engine():
        state["ev"] += 1
        return "act" if state["ev"] % 2 else "dve"

    def copy_op(en, out, in_, reads, writes):
        if en == "act":
            return P.op("act", lambda h: h.activation(out=out, in_=in_, func=AF.Copy), reads, writes)
        return P.op("dve", lambda h: h.tensor_copy(out=out, in_=in_), reads, writes)

    def mmf(out, lhsT, rhs, start, stop):
        return lambda h: h.matmul(out, lhsT, rhs, start=start, stop=stop)

    def trf(out, in_, ident):
        return lambda h: h.transpose(out, in_, ident)

    P.op("dve", lambda h: h.memset(ones_bf, 1.0), writes=[b_const])
    P.dma_add("sp", cond_s, cond_d, b_const)
    P.dma_add("sp", identf, idf_d, b_const)
    P.dma_add("pool", ident_bf, idf_d, b_const)
    P.dma_add("pool", ident8, id8_d, b_const)
    for l in range(2):
        P.dma_add("sp", bmods[l], bmod_d[l], b_const)
        P.dma_add("sp", ng[l][0], n1g_d[l], b_const)
        P.dma_add("sp", ng[l][1], n2g_d[l], b_const)
        P.dma_add("sp", qgb[l], qg_d[l], b_const)
        P.dma_add("sp", kgb[l], kg_d[l], b_const)
    P.op("act", lambda h: h.activation(out=s_bf, in_=cond_s, func=AF.Silu), reads=[b_const], writes=[b_const])

    P.barrier()
    xin = fv(R0, 16384).rearrange("p (t d) -> p t d", t=8)
    b_xin = [P.buf(f"xin{t}") for t in range(8)]
    xv = x_d.rearrange("(t p) d -> p t d", p=128)
    for t in range(8):
        P.dma("sp", xin[:, t, :], xv[:, t, :], writes=[b_xin[t]])

    mod_q = []
    for l_ in range(depth):
        for w_ in range(6):
            for b_ in range(16):
                mod_q.append((l_, w_, b_))
    mod_done = set()
    mod_state = {"in_attn": False, "hb": 0}
    b_mq = [[Buf(f"mod{l_}_{w_}") for w_ in range(6)] for l_ in range(2)]

    def mod_block():
        l_, w_, b_ = mod_q.pop(0)
        bp, ps = b_psO[0], psO[0]
        col0 = w_ * 2048 + b_ * 128
        i = state["mslot"] % 2
        state["mslot"] += 1
        sv = mslots[i].rearrange("p (k n) -> p k n", k=16)
        bs = b_mslot[i]
        P.dma("pool", sv, wmod_d[l_].rearrange("(kc p) n -> p kc n", p=128)[:, :, col0:col0 + 128], writes=[bs])
        fns = [mmf(ps[:, b_:b_ + 1], sv[:, kc, :], s_bf[:, kc:kc + 1], kc == 0, kc == 15) for kc in range(16)]
        P.pe_group(fns, reads=[bs, b_const], writes=[bp])
        if b_ == 15:
            mod_fin(l_, w_)

    def mod_fin(l_, w_):
        bp, ps = b_psO[0], psO[0]
        md = mods[l_]
        bm = bmods[l_]
        P.op("dve", lambda h: h.tensor_tensor(out=md[:, w_ * 16:(w_ + 1) * 16], in0=ps[:, 0:16], in1=bm[:, w_ * 16:(w_ + 1) * 16], op=ALU.add),
             reads=[bp, b_const], writes=[b_mq[l_][w_]])
        if w_ == 1:
            P.op("dve", lambda h: h.scalar_tensor_tensor(out=gm[l_][0], in0=md[:, 16:32], scalar=1.0, in1=ng[l_][0],
                                                          op0=ALU.add, op1=ALU.mult), reads=[b_const], writes=[b_mq[l_][1]])
        if w_ == 4:
            P.op("dve", lambda h: h.scalar_tensor_tensor(out=gm[l_][1], in0=md[:, 64:80], scalar=1.0, in1=ng[l_][1],
                                                          op0=ALU.add, op1=ALU.mult), reads=[b_const], writes=[b_mq[l_][4]])
        mod_done.add((l_, w_))

    def mod_unit_big(l_, w_):
        bp, ps = b_psO[0], psO[0]
        for b4 in range(4):
            col0 = w_ * 2048 + b4 * 512
            bs, sv = wload(wmod_d[l_].rearrange("(kc p) n -> p kc n", p=128)[:, :, col0:col0 + 512], "p (k n) -> p k n", k=16)
            fns = []
            for j in range(4):
                cc = b4 * 4 + j
                for kc in range(16):
                    fns.append(mmf(ps[:, cc:cc + 1], sv[:, kc, j * 128:(j + 1) * 128], s_bf[:, kc:kc + 1], kc == 0, kc == 15))
            P.pe_group(fns, reads=[bs, b_const], writes=[bp])
        mod_fin(l_, w_)
        mod_q[:] = [m for m in mod_q if not (m[0] == l_ and m[1] == w_)]

    def mod_require(l_, w_):
        while (l_, w_) not in mod_done:
            mod_block()

    def mod_finish_unit():
        while mod_q and mod_q[0][2] != 0:
            mod_block()

    def heavy_done():
        mod_state["hb"] += 1
        for _ in range(2):
            if mod_q and not mod_state["in_attn"]:
                mod_block()

    def mod_idle(n):
        return

    mod_unit_big(0, 0)
    mod_unit_big(0, 1)
    for th in range(2):
        for dc in range(16):
            bp, ps = next_psS()
            fns = [trf(ps[:, j * 128:(j + 1) * 128], xin[:, th * 4 + j, dc * 128:(dc + 1) * 128], identf) for j in range(4)]
            P.pe_group(fns, reads=[b_const] + b_xin[th * 4:th * 4 + 4], writes=[bp])
            copy_op(evac_engine(), XT[:, dc, th * 512:(th + 1) * 512], ps[:, :], [bp], [b_xT[dc]])

    O_HT = R0
    O_QN = R0 + 8192
    O_UT = R0 + 12288
    O_OT = R0 + 16384
    O_SC = R0 + 20480

    def norm_stats(l, sc_off):
        sq = [bv(sc_off + i * 256, 256) for i in range(3)]
        b_sq = [P.buf(f"sq{i}") for i in range(3)]
        k = 0
        for th in range(2):
            bp, ps = next_psS()
            for dc in range(16):
                i = k % 3
                k += 1
                sqi = sq[i]
                if dc % 8 in (0, 3, 6):
                    P.op("act", lambda h, sqi=sqi, dc=dc, th=th: h.activation(out=sqi, in_=XT[:, dc, th * 512:(th + 1) * 512], func=AF.Square),
                         reads=[b_xT[dc]], writes=[b_sq[i]])
                else:
                    P.op("dve", lambda h, sqi=sqi, dc=dc, th=th: h.tensor_tensor(out=sqi, in0=XT[:, dc, th * 512:(th + 1) * 512],
                                                                               in1=XT[:, dc, th * 512:(th + 1) * 512], op=ALU.mult),
                         reads=[b_xT[dc]], writes=[b_sq[i]])
                P.pe_group([mmf(ps[:, :], ones_bf, sqi, dc == 0, dc == 15)], reads=[b_sq[i], b_const], writes=[bp])
            rs = rstd[:, th * 512:(th + 1) * 512]
            P.op("act", lambda h, rs=rs, ps=ps: h.activation(out=rs, in_=ps[:, :], func=AF.Sqrt, bias=EPS, scale=1.0 / D),
                 reads=[bp], writes=[b_rstd])
            P.op("dve", lambda h, rs=rs: h.reciprocal(out=rs, in_=rs), reads=[b_rstd], writes=[b_rstd])

    def make_h(l, which, hT, b_hT, sc_off):
        g = gm[l][which]
        bmods_ = [b_mq[l][0], b_mq[l][1]] if which == 0 else [b_mq[l][3], b_mq[l][4]]
        sh = mods[l][:, (0 if which == 0 else 48):(16 if which == 0 else 64)]
        tmp = [fv(sc_off + i * 512, 512) for i in range(3)]
        b_tmp = [P.buf(f"tmp{i}") for i in range(3)]
        k = 0
        for th in range(2):
            for dc in range(16):
                i = k % 3
                k += 1
                tm = tmp[i]
                P.op("dve", lambda h, tm=tm, dc=dc, th=th: h.scalar_tensor_tensor(
                    out=tm, in0=XT[:, dc, th * 512:(th + 1) * 512], scalar=g[:, dc:dc + 1],
                    in1=rstd[:, th * 512:(th + 1) * 512], op0=ALU.mult, op1=ALU.mult),
                    reads=[b_xT[dc], b_rstd] + bmods_, writes=[b_tmp[i]])
                P.op("act", lambda h, tm=tm, dc=dc, th=th: h.activation(
                    out=hT[:, dc, th * 512:(th + 1) * 512], in_=tm, func=AF.Identity, bias=sh[:, dc:dc + 1], scale=1.0),
                    reads=[b_tmp[i]] + bmods_, writes=[b_hT[th][dc]])

    def layer(l):
        mod_require(l, 0)
        mod_require(l, 1)
        P.barrier()
        hT = bv(O_HT, 8192).rearrange("p (c t) -> p c t", c=16)
        b_hT = [[P.buf(f"hT{t_}_{c}") for c in range(16)] for t_ in range(2)]
        norm_stats(l, O_SC)
        mod_idle(4)
        make_h(l, 0, hT, b_hT, O_SC + 768)
        P.barrier()
        qn = bv(O_QN, 4096).rearrange("p (t f) -> p t f", t=8)
        b_qn = [P.buf(f"qn{t}") for t in range(8)]
        UT = bv(O_UT, 4096).rearrange("p (c t) -> p c t", c=8)
        b_UT = [P.buf(f"UT{c}") for c in range(8)]
        OT = bv(O_OT, 4096).rearrange("p (c t) -> p c t", c=8)
        b_OT = [P.buf(f"OT{c}") for c in range(8)]
        sc2 = O_SC
        sqf_ = [fv(sc2, 512), fv(sc2 + 512, 512)]
        tq_ = [fv(sc2 + 1024, 512), fv(sc2 + 1536, 512)]
        st_ = [fv(sc2 + 2048, 8), fv(sc2 + 2072, 8)]
        st2_ = [fv(sc2 + 2056, 8), fv(sc2 + 2080, 8)]
        st3_ = [fv(sc2 + 2064, 8), fv(sc2 + 2088, 8)]
        kst = [fv(sc2 + 2096 + i * 512, 512) for i in range(3)]
        b_sqf_ = [P.buf("sqf0"), P.buf("sqf1")]; b_tq_ = [P.buf("tq0"), P.buf("tq1")]; b_st_ = [P.buf("st0"), P.buf("st1")]
        nrm_i = [0]
        b_kst = [P.buf(f"kst{i}") for i in range(3)]
        kcount = [0]
        okv = ok_d[l].rearrange("(t p) f -> p t f", p=128)
        ovv = ov_d[l].rearrange("(t p) f -> p t f", p=128)
        winv = win_d[l].rearrange("(kc p) n -> p kc n", p=128)
        for part in range(3):
            for cb in range(2):
                col0 = part * 1024 + cb * 512
                bs, sv = wload(winv[:, :, col0:col0 + 512], "p (k n) -> p k n", k=16)
                for tt in range(8):
                    bp, ps = next_psS()
                    fns = [mmf(ps[:, :], hT[:, kc, tt * 128:(tt + 1) * 128], sv[:, kc, :], kc == 0, kc == 15) for kc in range(16)]
                    P.pe_group(fns, reads=[bs] + b_hT[tt // 4], writes=[bp])
                    pv = ps[:, :]
                    if part < 2:
                        ni = nrm_i[0] % 2
                        nrm_i[0] += 1
                        sqf = sqf_[ni]; tq = tq_[ni]; st = st_[ni]; st2 = st2_[ni]; st3 = st3_[ni]
                        b_sqf = b_sqf_[ni]; b_tq = b_tq_[ni]; b_st = b_st_[ni]
                        gbc = (qgb[l] if part == 0 else kgb[l]).unsqueeze(1).to_broadcast([128, 8, 64])
                        P.op("act", lambda h, pv=pv, sqf=sqf: h.activation(out=sqf, in_=pv, func=AF.Square), reads=[bp], writes=[b_sqf])
                        P.op("dve", lambda h, st=st, sqf=sqf: h.tensor_reduce(out=st, in_=sqf.rearrange("p (a b) -> p a b", a=8), axis=AX.X, op=ALU.add),
                             reads=[b_sqf], writes=[b_st])
                        P.op("act", lambda h, st=st, st2=st2: h.activation(out=st2, in_=st, func=AF.Sqrt, bias=EPS, scale=1.0 / 64), reads=[b_st], writes=[b_st])
                        P.op("dve", lambda h, st2=st2, st3=st3: h.reciprocal(out=st3, in_=st2), reads=[b_st], writes=[b_st])
                        P.op("dve", lambda h, pv=pv, tq=tq, st3=st3: h.tensor_tensor(out=tq.rearrange("p (a b) -> p a b", a=8),
                                                                     in0=pv.rearrange("p (a b) -> p a b", a=8),
                                                                     in1=st3.unsqueeze(2).to_broadcast([128, 8, 64]), op=ALU.mult),
                             reads=[bp, b_st], writes=[b_tq])
                        if part == 0:
                            P.op("dve", lambda h, tt=tt, cb=cb, gbc=gbc, tq=tq: h.tensor_tensor(
                                out=qn[:, tt, cb * 512:(cb + 1) * 512].rearrange("p (a b) -> p a b", a=8),
                                in0=tq.rearrange("p (a b) -> p a b", a=8), in1=gbc, op=ALU.mult),
                                reads=[b_tq, b_const], writes=[b_qn[tt]])
                        else:
                            i = kcount[0] % 3
                            kcount[0] += 1
                            ks = kst[i]
                            P.op("dve", lambda h, ks=ks, gbc=gbc, tq=tq: h.tensor_tensor(
                                out=ks.rearrange("p (a b) -> p a b", a=8),
                                in0=tq.rearrange("p (a b) -> p a b", a=8), in1=gbc, op=ALU.mult),
                                reads=[b_tq, b_const], writes=[b_kst[i]])
                            P.dma_add("sp", okv[:, tt, cb * 512:(cb + 1) * 512], ks, b_ok[l], reads=[b_kst[i]])
                    else:
                        i = kcount[0] % 3
                        kcount[0] += 1
                        ks = kst[i]
                        copy_op(evac_engine(), ks, pv, [bp], [b_kst[i]])
                        P.dma_add("sp", ovv[:, tt, cb * 512:(cb + 1) * 512], ks, b_ov[l], reads=[b_kst[i]])
                heavy_done()
        if stop == 'p2' and l == 1:
            return
        for cb in range(2):
            col0 = 3072 + cb * 512
            bs, sv = wload(winv[:, :, col0:col0 + 512], "p (k n) -> p k n", k=16)
            for j in range(4):
                uc = cb * 4 + j
                for th in range(2):
                    bp, ps = next_psS()
                    fns = [mmf(ps[:, :], sv[:, kc, j * 128:(j + 1) * 128], hT[:, kc, th * 512:(th + 1) * 512], kc == 0, kc == 15) for kc in range(16)]
                    P.pe_group(fns, reads=[bs] + b_hT[th], writes=[bp])
                    copy_op(evac_engine(), UT[:, uc, th * 512:(th + 1) * 512], ps[:, :], [bp], [b_UT[uc]])
            heavy_done()
        P.barrier()
        CT = bv(O_HT, 4096).rearrange("p (t n) -> p t n", t=8)
        NST = bv(O_HT + 4096, 4096).rearrange("p (t n) -> p t n", t=8)
        b_ct = P.buf("dft_ct"); b_nst = P.buf("dft_nst"); b_cs = P.buf("dft_cs")
        AB = bv(O_SC, 2048).rearrange("p (t n) -> p t n", t=8)
        CS = bv(O_SC + 2048, 512).rearrange("p (c n) -> p c n", c=2)
        b_AB = [P.buf(f"AB{t}") for t in range(8)]
        P.dma("pool", CS, cs_d.rearrange("(c p) n -> p c n", p=128), writes=[b_cs])
        P.dma("pool", CT, ct_d.rearrange("(t p) n -> p t n", p=128), writes=[b_ct])
        P.dma("pool", NST, nst_d.rearrange("(t p) n -> p t n", p=128), writes=[b_nst])
        mod_idle(2)
        for g in range(4):
            if g > 0:
                mod_idle(1)
            for tt in range(8):
                bp, ps = next_psS()
                fns = [mmf(ps[:, :], UT[:, 2 * g + c2, tt * 128:(tt + 1) * 128], CS[:, c2, :], c2 == 0, c2 == 1) for c2 in range(2)]
                P.pe_group(fns, reads=[b_cs, b_UT[2 * g], b_UT[2 * g + 1]], writes=[bp])
                copy_op(evac_engine(), AB[:, tt, :], ps[:, :], [bp], [b_AB[tt]])
            for j in range(2):
                for th in range(2):
                    bp, ps = next_psS()
                    fns = []
                    for tt in range(8):
                        fns.append(mmf(ps[:, :], AB[:, tt, j * 128:(j + 1) * 128], CT[:, tt, th * 512:(th + 1) * 512], tt == 0, False))
                        fns.append(mmf(ps[:, :], AB[:, tt, 256 + j * 128:256 + (j + 1) * 128], NST[:, tt, th * 512:(th + 1) * 512], False, tt == 7))
                    P.pe_group(fns, reads=[b_ct, b_nst] + b_AB, writes=[bp])
                    copy_op(evac_engine(), UT[:, 2 * g + j, th * 512:(th + 1) * 512], ps[:, :], [bp], [b_UT[2 * g + j]])
        FT = UT
        b_FT = b_UT
        if stop == 'p5' and l == 1:
            return
        mod_finish_unit()
        mod_state["in_attn"] = True
        P.barrier()
        o3 = O_HT
        KV = []
        vflat = []
        for i in range(2):
            kp = bv(o3, 640).rearrange("p (c f) -> p c f", c=10); o3 += 640
            vflat.append(bv(o3, 1280))
            vp = bv(o3, 1280).rearrange("p (c h f) -> p c h f", c=10, h=2); o3 += 1280
            KV.append((kp, vp))
        b_K = [P.buf("k0"), P.buf("k1")]
        b_Kc = [P.buf("kc0"), P.buf("kc1")]
        b_V = [[P.buf(f"v{i}{hh}") for hh in range(2)] for i in range(2)]
        b_Vc = [[P.buf(f"vc{i}{hh}") for hh in range(2)] for i in range(2)]
        qTa = []
        kTa = []
        for i in range(2):
            qTa.append(bv(o3, 512)); o3 += 512
            kTa.append(bv(o3, 640)); o3 += 640
        b_qTa = [P.buf("qTa0"), P.buf("qTa1")]
        b_kTa = [P.buf("kTa0"), P.buf("kTa1")]
        b_kTa_c = [P.buf("kTac0"), P.buf("kTac1")]
        b_qx = [P.buf("qx0"), P.buf("qx1")]
        b_kx = [P.buf("kx0"), P.buf("kx1")]
        NPT = 6
        PT = []
        for i in range(NPT):
            PT.append(bv(o3, 256)); o3 += 256
        b_PT = [P.buf(f"PT{i}") for i in range(NPT)]
        rd = fv(o3, 512); o3 += 512
        b_rd = P.buf("rd")
        assert o3 <= O_HT + 8192, o3 - O_HT
        tabb = [bv(O_SC, 960), bv(O_SC + 960, 960)]
        b_tab = [P.buf("tab0"), P.buf("tab1")]
        b_tab2 = [P.buf("tab0b"), P.buf("tab1b")]
        for i in range(2):
            P.op("dve", lambda h, i=i: h.memset(tabb[i], 0.0), writes=[b_tab[i], b_tab2[i]])
            P.op("dve", lambda h, i=i: h.memset(vflat[i], 1.0), writes=[b_V[i][0], b_V[i][1], b_Vc[i][0], b_Vc[i][1]])
            P.dma("pool", qTa[i][64:81, :], qx_d, writes=[b_qx[i]])
            P.dma("pool", kTa[i][64:81, :], kx_d, writes=[b_kx[i]])
        ckv = ck_d[l].rearrange("(c p) f -> p c f", p=128)
        cvv = cv_d[l].rearrange("(c p) f -> p c f", p=128)
        ptc = [0]
        occ = [0]
        KCS = {0: [0, 1, 2, 3, 4, 5, 8, 9], 1: [2, 3, 4, 5, 6, 7, 8, 9]}

        def load_pair(j):
            kp, vp = KV[j % 2]
            i = j % 2
            P.dma("pool", kp[:, 0:8, :], okv[:, :, j * 128:(j + 1) * 128], reads=[b_ok[l]], writes=[b_K[i]])
            P.dma("pool", kp[:, 8:10, :], ckv[:, :, j * 128:(j + 1) * 128], writes=[b_Kc[i]])
            for hh in range(2):
                c0 = j * 128 + hh * 64
                P.dma("pool", vp[:, 0:8, hh, 0:64], ovv[:, :, c0:c0 + 64], reads=[b_ov[l]], writes=[b_V[i][hh]])
                P.dma("pool", vp[:, 8:10, hh, 0:64], cvv[:, :, c0:c0 + 64], writes=[b_Vc[i][hh]])

        def prep_head(hd):
            j, hh = divmod(hd, 2)
            hb = hd % 2
            kp, vp = KV[j % 2]
            tb = tabb[hb].rearrange("p (e c) -> p e c", e=30)
            P.dma("pool", tb[0:64, 7:22, :], tab_d[l, hd].rearrange("p (e c) -> p e c", e=15), writes=[b_tab[hb]])
            P.dma("pool", tb[64:128, 8:23, :], tab_d[l, hd].rearrange("p (e c) -> p e c", e=15), writes=[b_tab2[hb]])
            if hh == 1 and j + 1 < 8:
                load_pair(j + 1)
            b_kTc = b_kTa_c[hb]
            bt, pt = next_psT()
            fns = [trf(pt[0:64, c * 128:(c + 1) * 128], kp[:, 8 + c, hh * 64:(hh + 1) * 64], ident_bf) for c in range(2)]
            P.pe_group(fns, reads=[b_const, b_Kc[j % 2]], writes=[bt])
            copy_op("act", kTa[hb][0:64, 1024:1280], pt[0:64, 0:256], [bt], [b_kTc])
            bt, pt = next_psT()
            fns = [trf(pt[0:64, tt * 128:(tt + 1) * 128], qn[:, tt, hd * 64:(hd + 1) * 64], ident_bf) for tt in range(8)]
            P.pe_group(fns, reads=[b_const] + b_qn, writes=[bt])
            copy_op("dve", qTa[hb][0:64, :], pt[0:64, :], [bt], [b_qTa[hb]])
            bt, pt = next_psT()
            fns = [trf(pt[0:64, tt * 128:(tt + 1) * 128], kp[:, tt, hh * 64:(hh + 1) * 64], ident_bf) for tt in range(8)]
            P.pe_group(fns, reads=[b_const, b_K[j % 2]], writes=[bt])
            copy_op("dve", kTa[hb][0:64, 0:1024], pt[0:64, :], [bt], [b_kTa[hb]])

        def attend(hd, th):
            j, hh = divmod(hd, 2)
            hb = hd % 2
            kp, vp = KV[j % 2]
            oi = occ[0] % 2
            occ[0] += 1
            bo, pso = b_psO[oi], psO[oi]
            kcs = KCS[th]

            def S(kc):
                bp, ps = next_psS()
                fns = [mmf(ps[:, :], kTa[hb][0:81, kc * 128:(kc + 1) * 128], qTa[hb][0:81, th * 512:(th + 1) * 512], True, kc >= 8)]
                rds = [b_kTa[hb], b_kTa_c[hb], b_qTa[hb], b_qx[hb], b_kx[hb]]
                if kc < 8:
                    s0 = 8 * th - 2 * kc + 14
                    fns.append(mmf(ps[:, :], ident8, tabb[hb][:, s0 * 64:(s0 + 8) * 64], False, True))
                    rds += [b_tab[hb], b_tab2[hb], b_const]
                P.pe_group(fns, reads=rds, writes=[bp])
                i = ptc[0] % NPT
                ptc[0] += 1
                pti = PT[i]
                P.op("act", lambda h, pti=pti, ps=ps: h.activation(out=pti, in_=ps[:, :], func=AF.Exp, bias=NEGB, scale=0.125),
                     reads=[bp], writes=[b_PT[i]])
                return i

            def PV(n, kc, i):
                fns = [mmf(pso[:, :], vp[:, kc, hh, :], PT[i], n == 0, n == len(kcs) - 1)]
                P.pe_group(fns, reads=[b_PT[i], b_V[j % 2][hh], b_Vc[j % 2][hh]], writes=[bo])

            idx = {}
            for n in range(3):
                idx[n] = S(kcs[n])
            for n in range(len(kcs)):
                PV(n, kcs[n], idx[n])
                if n + 3 < len(kcs):
                    idx[n + 3] = S(kcs[n + 3])
            P.op("dve", lambda h, pso=pso: h.reciprocal(out=rd[0:64, :], in_=pso[64:128, :]), reads=[bo], writes=[b_rd])
            P.op("dve", lambda h, pso=pso, hh=hh, j=j, th=th: h.tensor_tensor(
                out=OT[hh * 64:(hh + 1) * 64, j, th * 512:(th + 1) * 512], in0=pso[0:64, :], in1=rd[0:64, :], op=ALU.mult),
                reads=[bo, b_rd], writes=[b_OT[j]])

        load_pair(0)
        prep_head(0)
        for hd in range(16):
            attend(hd, 0)
            if hd + 1 < 16:
                prep_head(hd + 1)
            attend(hd, 1)
        if stop == 'p3' and l == 1:
            mod_state['in_attn'] = False
            return
        mod_state["in_attn"] = False
        mod_require(l, 2)
        P.barrier()
        hT = bv(O_HT, 8192).rearrange("p (c t) -> p c t", c=16)
        b_hT = [[P.buf(f"hTb{t_}_{c}") for c in range(16)] for t_ in range(2)]
        mod_idle(2)
        make_h(l, 0, hT, b_hT, O_SC)
        naT = bv(O_QN, 2048).rearrange("p (c t) -> p c t", c=4)
        fnT = bv(O_QN + 2048, 2048).rearrange("p (c t) -> p c t", c=4)
        b_na = [P.buf(f"na{c}") for c in range(4)]
        b_fn = [P.buf(f"fn{c}") for c in range(4)]
        sg = [bv(O_SC + 1536 + i * 256, 256) for i in range(3)]
        b_sg = [P.buf(f"sg{i}") for i in range(3)]
        sgc = [0]
        g1 = mods[l][:, 32:48]
        wnav = wna_d[l].rearrange("(kc p) n -> p kc n", p=128)
        wfnv = wfn_d[l].rearrange("(kc p) n -> p kc n", p=128)
        wgv = wgate_d[l].rearrange("(kc p) n -> p kc n", p=128)
        for c4 in range(4):
            for (wv, src, b_src, dst, b_dst) in ((wnav, OT, b_OT, naT, b_na), (wfnv, FT, b_FT, fnT, b_fn)):
                bs, sv = wload(wv[:, :, c4 * 512:(c4 + 1) * 512], "p (k n) -> p k n", nelem=4096, k=8)
                for jj in range(4):
                    for th in range(2):
                        bp, ps = next_psS()
                        fns = [mmf(ps[:, :], sv[:, kc, jj * 128:(jj + 1) * 128], src[:, kc, th * 512:(th + 1) * 512], kc == 0, kc == 7) for kc in range(8)]
                        P.pe_group(fns, reads=[bs] + b_src, writes=[bp])
                        copy_op(evac_engine(), dst[:, jj, th * 512:(th + 1) * 512], ps[:, :], [bp], [b_dst[jj]])
                heavy_done()
            for gi in range(2):
                col0 = gi * 2048 + c4 * 512
                bs, sv = wload(wgv[:, :, col0:col0 + 512], "p (k n) -> p k n", k=16)
                for jj in range(4):
                    for th in range(2):
                        bp, ps = next_psS()
                        fns = [mmf(ps[:, :], sv[:, kc, jj * 128:(jj + 1) * 128], hT[:, kc, th * 512:(th + 1) * 512], kc == 0, kc == 15) for kc in range(16)]
                        P.pe_group(fns, reads=[bs] + b_hT[th], writes=[bp])
                        i = sgc[0] % 3
                        sgc[0] += 1
                        sgi = sg[i]
                        P.op("act", lambda h, sgi=sgi, ps=ps: h.activation(out=sgi, in_=ps[:, :], func=AF.Sigmoid), reads=[bp], writes=[b_sg[i]])
                        nsl = naT[:, jj, th * 512:(th + 1) * 512]
                        if gi == 0:
                            P.op("dve", lambda h, sgi=sgi, nsl=nsl: h.tensor_tensor(out=nsl, in0=sgi, in1=nsl, op=ALU.mult),
                                 reads=[b_sg[i], b_na[jj]], writes=[b_na[jj]])
                        else:
                            fsl = fnT[:, jj, th * 512:(th + 1) * 512]
                            P.op("dve", lambda h, sgi=sgi, fsl=fsl: h.tensor_tensor(out=sgi, in0=sgi, in1=fsl, op=ALU.mult),
                                 reads=[b_fn[jj]], writes=[b_sg[i]])
                            P.op("dve", lambda h, sgi=sgi, nsl=nsl: h.tensor_tensor(out=nsl, in0=sgi, in1=nsl, op=ALU.add),
                                 reads=[b_sg[i], b_na[jj]], writes=[b_na[jj]])
                heavy_done()
            wov = wo_d[l][c4 * 512:(c4 + 1) * 512, :].rearrange("(kc p) n -> p kc n", p=128)
            bs, sv = wload(wov, "p (k n) -> p k n", k=4)
            for co in range(16):
                for th in range(2):
                    bp, ps = next_psS()
                    fns = [mmf(ps[:, :], sv[:, kc, co * 128:(co + 1) * 128], naT[:, kc, th * 512:(th + 1) * 512], kc == 0, kc == 3) for kc in range(4)]
                    P.pe_group(fns, reads=[bs] + b_na, writes=[bp])
                    xs = XT[:, co, th * 512:(th + 1) * 512]
                    P.op("dve", lambda h, xs=xs, ps=ps, co=co: h.scalar_tensor_tensor(
                        out=xs, in0=ps[:, :], scalar=g1[:, co:co + 1], in1=xs, op0=ALU.mult, op1=ALU.add),
                        reads=[bp, b_mq[l][2]], writes=[b_xT[co]])
            heavy_done()
        if stop == 'p6' and l == 1:
            return
        mod_require(l, 3)
        mod_require(l, 4)
        P.barrier()
        hT = bv(O_HT, 8192).rearrange("p (c t) -> p c t", c=16)
        b_hT = [[P.buf(f"h2T{t_}_{c}") for c in range(16)] for t_ in range(2)]
        norm_stats(l, O_SC)
        mod_idle(4)
        make_h(l, 1, hT, b_hT, O_SC + 768)
        mod_require(l, 5)
        actT = bv(O_QN, 4096).rearrange("p (c t) -> p c t", c=8)
        b_act = [P.buf(f"act{c}") for c in range(8)]
        a_sb = bv(O_OT, 2048).rearrange("p (c t) -> p c t", c=4)
        b_asb = [P.buf(f"asb{c}") for c in range(4)]
        sgf = [fv(O_SC + 2304 + i * 512, 512) for i in range(3)]
        b_sgf = [P.buf(f"sgf{i}") for i in range(3)]
        g2 = mods[l][:, 80:96]
        wguv = wgu_d[l].rearrange("(kc p) n -> p kc n", p=128)
        sgc = [0]
        nf = DFF // 512
        f = 0
        while f < nf:
            nfg = min(2, nf - f)
            for fi in range(nfg):
                ff = f + fi
                bs, sv = wload(wguv[:, :, ff * 512:(ff + 1) * 512], "p (k n) -> p k n", k=16)
                for j in range(4):
                    for th in range(2):
                        bp, ps = next_psS()
                        fns = [mmf(ps[:, :], sv[:, kc, j * 128:(j + 1) * 128], hT[:, kc, th * 512:(th + 1) * 512], kc == 0, kc == 15) for kc in range(16)]
                        P.pe_group(fns, reads=[bs] + b_hT[th], writes=[bp])
                        copy_op(evac_engine(), a_sb[:, j, th * 512:(th + 1) * 512], ps[:, :], [bp], [b_asb[j]])
                heavy_done()
                bs, sv = wload(wguv[:, :, DFF + ff * 512:DFF + (ff + 1) * 512], "p (k n) -> p k n", k=16)
                for j in range(4):
                    ci = fi * 4 + j
                    for th in range(2):
                        bp, ps = next_psS()
                        fns = [mmf(ps[:, :], sv[:, kc, j * 128:(j + 1) * 128], hT[:, kc, th * 512:(th + 1) * 512], kc == 0, kc == 15) for kc in range(16)]
                        P.pe_group(fns, reads=[bs] + b_hT[th], writes=[bp])
                        i = sgc[0] % 3
                        sgc[0] += 1
                        sgi = sgf[i]
                        P.op("act", lambda h, sgi=sgi, ps=ps: h.activation(out=sgi, in_=ps[:, :], func=AF.Silu), reads=[bp], writes=[b_sgf[i]])
                        P.op("dve", lambda h, sgi=sgi, j=j, ci=ci, th=th: h.tensor_tensor(
                            out=actT[:, ci, th * 512:(th + 1) * 512], in0=sgi, in1=a_sb[:, j, th * 512:(th + 1) * 512], op=ALU.mult),
                            reads=[b_sgf[i], b_asb[j]], writes=[b_act[ci]])
                heavy_done()
            ng_ = nfg * 4
            wdv = wdn_d[l][f * 512:f * 512 + ng_ * 128, :].rearrange("(kc p) n -> p kc n", p=128)
            ncols = 8192 // ng_
            for cbd in range(D // ncols):
                bs, sv = wload(wdv[:, :, cbd * ncols:(cbd + 1) * ncols], "p (k n) -> p k n", k=ng_)
                for cq in range(ncols // 128):
                    co = cbd * (ncols // 128) + cq
                    for th in range(2):
                        bp, ps = next_psS()
                        fns = [mmf(ps[:, :], sv[:, kc, cq * 128:(cq + 1) * 128], actT[:, kc, th * 512:(th + 1) * 512], kc == 0, kc == ng_ - 1) for kc in range(ng_)]
                        P.pe_group(fns, reads=[bs] + b_act[:ng_], writes=[bp])
                        xs = XT[:, co, th * 512:(th + 1) * 512]
                        P.op("dve", lambda h, xs=xs, ps=ps, co=co: h.scalar_tensor_tensor(
                            out=xs, in0=ps[:, :], scalar=g2[:, co:co + 1], in1=xs, op0=ALU.mult, op1=ALU.add),
                            reads=[bp, b_mq[l][5]], writes=[b_xT[co]])
                heavy_done()
            f += nfg

    for l in range(depth):
        layer(l)

    P.barrier()
    ost = [fv(R0 + i * 512, 512) for i in range(4)]
    b_ost = [P.buf(f"ost{i}") for i in range(4)]
    b_oy = Buf("oy")
    k = 0
    for tt in range(8):
        for dq in range(4):
            bp, ps = next_psS()
            fns = [trf(ps[:, j * 128:(j + 1) * 128], XT[:, dq * 4 + j, tt * 128:(tt + 1) * 128], identf) for j in range(4)]
            P.pe_group(fns, reads=[b_const] + b_xT[dq * 4:dq * 4 + 4], writes=[bp])
            i = k % 4
            k += 1
            copy_op(evac_engine(), ost[i], ps[:, :], [bp], [b_ost[i]])
            P.dma_add("sp", oy_d[tt * 128:(tt + 1) * 128, dq * 512:(dq + 1) * 512], ost[i], b_oy, reads=[b_ost[i]])
    sp = P.eng["sp"]
    for t in sp.dlast:
        if t is not None:
            P.need(sp, t)

    sems = {}
    for n in ("pe", "act", "dve", "pool", "sp"):
        sems[("e", n)] = es.enter_context(nc.semaphore(f"s_{n}"))
    for q in ("pool", "sp"):
        for k in range(NDS):
            sems[("d", q, k)] = es.enter_context(nc.semaphore(f"d_{q}{k}"))

    def replay(e, h):
        own = sems[("e", e.name)]
        for o_ in e.ops:
            if o_[0] == "wait":
                h.wait_ge(sems[o_[1]], o_[2])
            elif o_[0] == "op":
                ins = o_[1](h)
                if o_[2]:
                    ins.then_inc(own, 1)
            else:
                h.dma_start(out=o_[1], in_=o_[2]).then_inc(sems[o_[3]], 16)

    with nc.Block() as block:
        @block.tensor
        def _(h):
            replay(P.eng["pe"], h)

        @block.scalar
        def _(h):
            replay(P.eng["act"], h)

        @block.vector
        def _(h):
            replay(P.eng["dve"], h)

        @block.gpsimd
        def _(h):
            replay(P.eng["pool"], h)

        @block.sync
        def _(h):
            replay(P.eng["sp"], h)
    es.close()
    return nc


def _host_consts():
    identf = np.eye(128, dtype=np.float32)
    ident8 = (8.0 * np.eye(128)).astype(np.float32)
    ch = np.arange(256)
    ang = 2.0 * np.pi * np.outer(ch, ch) / 256.0
    cs = np.concatenate([np.cos(ang), np.sin(ang)], axis=1) / 16.0
    return identf, ident8, cs.astype(np.float32)


def _dft_tables(seq):
    t = np.arange(seq)
    ang = 2.0 * np.pi * np.outer(t, t) / seq
    c = np.cos(ang) / np.sqrt(seq)
    s = -np.sin(ang) / np.sqrt(seq)
    nb = T // seq
    ct = np.zeros((T, T), np.float32)
    nst = np.zeros((T, T), np.float32)
    for b in range(nb):
        ct[b * seq:(b + 1) * seq, b * seq:(b + 1) * seq] = c
        nst[b * seq:(b + 1) * seq, b * seq:(b + 1) * seq] = s
    return ct, nst


def _mask_feats(sample):
    qx = np.zeros((17, 1024), np.float32)
    kx = np.zeros((17, 1280), np.float32)
    rows = np.arange(1024) // 64
    for j in range(16):
        kx[j, :1024] = (rows == j)
    kx[16, 1024:] = 1.0
    if sample:
        rs = np.clip(rows - 4, 0, 8)
        for j in range(16):
            qx[j] = 512.0 * ((rs <= j) & (j <= rs + 7))
        qx[16] = 512.0
    else:
        for j in range(16):
            qx[j] = 512.0 * ((rows // 4) == (j // 4))
    return qx, kx


def _bias_table(rpb):
    cp = np.arange(64)[:, None]
    c = np.arange(64)[None, :]
    ws = np.clip(c - 8, 0, 48)
    inwin = (cp >= ws) & (cp < ws + 16)
    dc = np.clip(cp - c + 15, 0, 30)
    tab = np.empty((2, 16, 64, 15, 64), np.float32)
    for e in range(15):
        g = rpb[:, :, 14 - e, :][:, :, dc]
        tab[:, :, :, e, :] = np.where(inwin[None, None], g, np.float32(NEGB))
    return tab.reshape(2, 16, 64, 15 * 64)


_NC_CACHE = {}


def kernel(x_prompt, x_sample, cache_k, cache_v, c, c_ctx, w_mod, b_mod, norm1_g, norm2_g,
           w_in, q_norm_g, k_norm_g, rpb, w_na_proj, w_fnet_proj, w_gate, w_o, w_gate_up, w_down):
    f = lambda a: np.ascontiguousarray(np.asarray(a, dtype=np.float32))
    x_prompt, x_sample, cache_k, cache_v, c, c_ctx = map(f, (x_prompt, x_sample, cache_k, cache_v, c, c_ctx))
    w_mod, b_mod, norm1_g, norm2_g, w_in, q_norm_g, k_norm_g, rpb = map(f, (w_mod, b_mod, norm1_g, norm2_g, w_in, q_norm_g, k_norm_g, rpb))
    w_na_proj, w_fnet_proj, w_gate, w_o, w_gate_up, w_down = map(f, (w_na_proj, w_fnet_proj, w_gate, w_o, w_gate_up, w_down))

    identf, ident8, cs = _host_consts()
    ct_s, nst_s = _dft_tables(1024)
    ct_p, nst_p = _dft_tables(256)
    qx_s, kx_s = _mask_feats(True)
    qx_p, kx_p = _mask_feats(False)
    tab_s = _bias_table(rpb)
    tab_p = np.zeros_like(tab_s)
    bmod_l = f(b_mod.reshape(2, 96, 128).transpose(0, 2, 1))
    n1g_l = f(norm1_g.reshape(2, 16, 128).transpose(0, 2, 1))
    n2g_l = f(norm2_g.reshape(2, 16, 128).transpose(0, 2, 1))
    qg_l = f(np.broadcast_to(q_norm_g[:, None, :], (2, 128, 64)))
    kg_l = f(np.broadcast_to(k_norm_g[:, None, :], (2, 128, 64)))
    zkv = np.zeros((2, 256, 1024), np.float32)
    shared = {"bmod": bmod_l, "n1g": n1g_l, "n2g": n2g_l, "qg": qg_l, "kg": kg_l, "w_mod": w_mod, "w_in": w_in,
              "w_na": w_na_proj, "w_fn": w_fnet_proj, "w_gate": w_gate, "w_o": w_o, "w_gu": w_gate_up, "w_dn": w_down,
              "cs": cs, "identf": identf, "ident8": ident8}
    in_maps = []
    for core in range(8):
        m = dict(shared)
        if core < 4:
            m["x"] = f(x_prompt[4 * core:4 * core + 4].reshape(T, D))
            cond = c_ctx
            m["ck"] = zkv; m["cv"] = zkv
            m["tab"] = tab_p; m["qx"] = qx_p; m["kx"] = kx_p; m["ct"] = ct_p; m["nst"] = nst_p
        else:
            b = core - 4
            m["x"] = f(x_sample[b])
            cond = c[b]
            m["ck"] = f(cache_k[b].reshape(2, 256, 1024)); m["cv"] = f(cache_v[b].reshape(2, 256, 1024))
            m["tab"] = tab_s; m["qx"] = qx_s; m["kx"] = kx_s; m["ct"] = ct_s; m["nst"] = nst_s
        m["cond"] = f(cond.reshape(16, 128).T)
        in_maps.append(m)
    if "nc" not in _NC_CACHE:
        _NC_CACHE["nc"] = build_program(2)
    nc = _NC_CACHE["nc"]
    res = run_bass_kernel_spmd(nc, in_maps, core_ids=list(range(8)))
    rs = res.results
    y_prompt = np.concatenate([rs[i]["oy"].reshape(4, 256, D) for i in range(4)], axis=0).astype(np.float32)
    y_sample = np.stack([rs[4 + b]["oy"] for b in range(4)], axis=0).astype(np.float32)
    new_k = np.concatenate([rs[i]["ok"].reshape(2, 4, 256, 16, 64).transpose(1, 0, 2, 3, 4) for i in range(4)], axis=0).astype(np.float32)
    new_v = np.concatenate([rs[i]["ov"].reshape(2, 4, 256, 16, 64).transpose(1, 0, 2, 3, 4) for i in range(4)], axis=0).astype(np.float32)
    return (y_prompt, y_sample, np.ascontiguousarray(new_k), np.ascontiguousarray(new_v))
```
